# Optimizing a Trainium2 kernel written in Bass

```python
import functools
import jax, jax.numpy as jnp
from jax import lax
import numpy as np

D_MODEL = 1024
BATCH = 32
SEQ = 2048
DEPTH = 1
DEC_BATCH = 32
DEC_SEQ = 64
PAST_LEN = 4096

CHUNK = 64
N_HEADS = 8
HEAD_DIM = 64
ATT_DIM = N_HEADS * HEAD_DIM
BAND_CHUNKS = 8
BAND_PAST = BAND_CHUNKS * CHUNK
MAX_REL = 128
CONV_DIM = 512
CONV_W = 3
D_FF = -(-8 * D_MODEL // (3 * 256)) * 256
IN_DIM = 3 * CONV_DIM + 3 * ATT_DIM + 2 * D_MODEL
RMS_EPS = 1e-6
NEG_INF = -1e30

kernel_name = "chunk_stream_gated_conv_band_attn_hybrid"


def rmsnorm(x, g):
    xf = x.astype(jnp.float32)
    y = xf * lax.rsqrt(jnp.mean(xf * xf, axis=-1, keepdims=True) + RMS_EPS)
    return y.astype(x.dtype) * g


def causal_conv(u, prefix, w):
    L = u.shape[1]
    up = jnp.concatenate([prefix, u], axis=1)
    y = w[0] * up[:, 0:L]
    for j in range(1, CONV_W):
        y = y + w[j] * up[:, j:j + L]
    return y, up[:, -(CONV_W - 1):]


def band_attend(q, k, v, q_pos, k_pos, rel_bias):
    s = jnp.einsum('bqhd,bkhd->bhqk', q.astype(jnp.float32), k.astype(jnp.float32)) * (HEAD_DIM ** -0.5)
    rel = q_pos[:, None] - k_pos[None, :]
    bias = rel_bias[:, jnp.clip(rel, -MAX_REL, MAX_REL) + MAX_REL].astype(jnp.float32)
    qc = q_pos // CHUNK
    kc = k_pos // CHUNK
    allowed = ((k_pos[None, :] >= 0) & (kc[None, :] <= qc[:, None])
               & (kc[None, :] >= qc[:, None] - BAND_CHUNKS))
    s = jnp.where(allowed[None, None], s + bias[None], NEG_INF)
    p = jax.nn.softmax(s, axis=-1)
    return jnp.einsum('bhqk,bkhd->bqhd', p, v.astype(jnp.float32)).astype(v.dtype)


def prompt_band_attention(q, k, v, rel_bias):
    B, L, H, Dh = q.shape
    nc = L // CHUNK
    pad = jnp.zeros((B, BAND_PAST, H, Dh), k.dtype)
    kp = jnp.concatenate([pad, k], axis=1)
    vp = jnp.concatenate([pad, v], axis=1)
    qcs = q.reshape(B, nc, CHUNK, H, Dh).transpose(1, 0, 2, 3, 4)

    def one_chunk(args):
        c, qb = args
        start = c * CHUNK
        kb = lax.dynamic_slice_in_dim(kp, start, BAND_PAST + CHUNK, axis=1)
        vb = lax.dynamic_slice_in_dim(vp, start, BAND_PAST + CHUNK, axis=1)
        q_pos = start + jnp.arange(CHUNK)
        k_pos = start - BAND_PAST + jnp.arange(BAND_PAST + CHUNK)
        return band_attend(qb, kb, vb, q_pos, k_pos, rel_bias)

    out = lax.map(one_chunk, (jnp.arange(nc), qcs))
    return out.transpose(1, 0, 2, 3, 4).reshape(B, L, H, Dh)


def prompt_attn(l, q, k, v, rel_bias):
    out = prompt_band_attention(q, k, v, rel_bias[l])
    rows = min(BAND_PAST, q.shape[1])
    return out, k[:, -rows:], v[:, -rows:]


def sample_attn(l, q, k, v, rel_bias, cache_k, cache_v):
    rows = cache_k.shape[2]
    Lq = q.shape[1]
    k_all = jnp.concatenate([cache_k[l], k], axis=1)
    v_all = jnp.concatenate([cache_v[l], v], axis=1)
    q_pos = PAST_LEN + jnp.arange(Lq)
    k_pos = PAST_LEN - rows + jnp.arange(rows + Lq)
    out = band_attend(q, k_all, v_all, q_pos, k_pos, rel_bias[l])
    return out, k_all[:, -rows:], v_all[:, -rows:]


def mixer(xn, conv_prefix, attn_fn, w_in, w_conv, w_conv_out, w_attn_out, w_out):
    B, L, _ = xn.shape
    proj = xn @ w_in
    splits = [CONV_DIM, 2 * CONV_DIM, 3 * CONV_DIM,
              3 * CONV_DIM + ATT_DIM, 3 * CONV_DIM + 2 * ATT_DIM, 3 * CONV_DIM + 3 * ATT_DIM,
              3 * CONV_DIM + 3 * ATT_DIM + D_MODEL]
    cb, cc, ch, q, k, v, ga, gb = jnp.split(proj, splits, axis=-1)
    cv, conv_state = causal_conv(cc * ch, conv_prefix, w_conv)
    ya = (cb * cv) @ w_conv_out
    heads = lambda t: t.reshape(B, L, N_HEADS, HEAD_DIM)
    o, new_k, new_v = attn_fn(heads(q), heads(k), heads(v))
    yb = o.reshape(B, L, ATT_DIM) @ w_attn_out
    merged = jax.nn.sigmoid(ga) * ya + jax.nn.sigmoid(gb) * yb
    return merged @ w_out, conv_state, new_k, new_v


def swiglu(x, w1, w3, w2):
    return (jax.nn.silu(x @ w1) * (x @ w3)) @ w2


def trunk(x, conv_prefix, attn_fn, g_mix, w_in, w_conv, w_conv_out, w_attn_out, w_out,
          g_ffn, w_ff1, w_ff3, w_ff2, g_final):
    convs, ks, vs = [], [], []
    for l in range(DEPTH):
        m, cs, nk, nv = mixer(rmsnorm(x, g_mix[l]), conv_prefix[l], functools.partial(attn_fn, l),
                              w_in[l], w_conv[l], w_conv_out[l], w_attn_out[l], w_out[l])
        x = x + m
        x = x + swiglu(rmsnorm(x, g_ffn[l]), w_ff1[l], w_ff3[l], w_ff2[l])
        convs.append(cs)
        ks.append(nk)
        vs.append(nv)
    return rmsnorm(x, g_final), jnp.stack(convs), jnp.stack(ks), jnp.stack(vs)


def setup_inputs(seed: int = 0) -> dict:
    key = jax.random.key(seed)
    ks = jax.random.split(key, 20)
    kv_rows = min(BAND_PAST, PAST_LEN)
    n = jax.random.normal
    f32 = jnp.float32
    return {
        "x_prompt": n(ks[0], (BATCH, SEQ, D_MODEL), f32),
        "x_sample": n(ks[1], (DEC_BATCH, DEC_SEQ, D_MODEL), f32),
        "state_conv": n(ks[2], (DEPTH, DEC_BATCH, CONV_W - 1, CONV_DIM), f32),
        "cache_k": n(ks[3], (DEPTH, DEC_BATCH, kv_rows, N_HEADS, HEAD_DIM), f32),
        "cache_v": n(ks[4], (DEPTH, DEC_BATCH, kv_rows, N_HEADS, HEAD_DIM), f32),
        "g_mix": 1.0 + 0.1 * n(ks[5], (DEPTH, D_MODEL), f32),
        "w_in": n(ks[6], (DEPTH, D_MODEL, IN_DIM), f32) * D_MODEL ** -0.5,
        "w_conv": n(ks[7], (DEPTH, CONV_W, CONV_DIM), f32) * CONV_W ** -0.5,
        "w_conv_out": n(ks[8], (DEPTH, CONV_DIM, D_MODEL), f32) * CONV_DIM ** -0.5,
        "rel_bias": 0.5 * n(ks[9], (DEPTH, N_HEADS, 2 * MAX_REL + 1), f32),
        "w_attn_out": n(ks[10], (DEPTH, ATT_DIM, D_MODEL), f32) * ATT_DIM ** -0.5,
        "w_out": n(ks[11], (DEPTH, D_MODEL, D_MODEL), f32) * D_MODEL ** -0.5,
        "g_ffn": 1.0 + 0.1 * n(ks[12], (DEPTH, D_MODEL), f32),
        "w_ff1": n(ks[13], (DEPTH, D_MODEL, D_FF), f32) * D_MODEL ** -0.5,
        "w_ff3": n(ks[14], (DEPTH, D_MODEL, D_FF), f32) * D_MODEL ** -0.5,
        "w_ff2": n(ks[15], (DEPTH, D_FF, D_MODEL), f32) * D_FF ** -0.5,
        "g_final": 1.0 + 0.1 * n(ks[16], (D_MODEL,), f32),
    }


def reference(x_prompt, x_sample, state_conv, cache_k, cache_v, g_mix, w_in, w_conv, w_conv_out,
              rel_bias, w_attn_out, w_out, g_ffn, w_ff1, w_ff3, w_ff2, g_final):
    zero_prefix = jnp.zeros((DEPTH, x_prompt.shape[0], CONV_W - 1, CONV_DIM), x_prompt.dtype)
    y_prompt, conv_p, k_p, v_p = trunk(
        x_prompt, zero_prefix, functools.partial(prompt_attn, rel_bias=rel_bias),
        g_mix, w_in, w_conv, w_conv_out, w_attn_out, w_out, g_ffn, w_ff1, w_ff3, w_ff2, g_final)
    y_sample, conv_s, k_s, v_s = trunk(
        x_sample, state_conv,
        functools.partial(sample_attn, rel_bias=rel_bias, cache_k=cache_k, cache_v=cache_v),
        g_mix, w_in, w_conv, w_conv_out, w_attn_out, w_out, g_ffn, w_ff1, w_ff3, w_ff2, g_final)
    return (y_prompt, y_sample, conv_p, k_p, v_p, conv_s, k_s, v_s)
```

```python
import numpy as np
from contextlib import ExitStack
import concourse.bass as bass
import concourse.mybir as mybir
from concourse.bass_utils import run_bass_kernel_spmd

F32 = mybir.dt.float32
BF16 = mybir.dt.bfloat16
AF = mybir.ActivationFunctionType
ALU = mybir.AluOpType

NCORES = 8
D = 1024
DFF = 2816
NH = 8
CD = 512
AD = 512
IN_DIM = 5120
SEQ = 2048
PB = 4
SB = 4
SL = 64
KVR = 512
RBW = 768
NS = 4
NSCR = 5
EPS = 1e-6
MASKV = -30000.0
import os as _os
DBG_JMIN = int(_os.environ.get('JMIN', '0'))
DBG_ATTN = int(_os.environ.get('ATTN', '9999'))
DBG_PIPE = int(_os.environ.get('PIPE', '1'))
DBG_NEWATT = int(_os.environ.get('NEWATT', '1'))


class Res:
    __slots__ = ("w", "rd")

    def __init__(self):
        self.w = None
        self.rd = {}


class Op:
    __slots__ = ("eng", "fn", "deps", "signal", "count", "idx", "is_dma", "semkey", "dma_val")


class Prog:
    ENGS = ("pe", "act", "dve", "pool", "sp")

    def __init__(self):
        self.ops = {e: [] for e in self.ENGS}
        self.dma_last = {}
        self.dma_cnt = {}
        self.out_keys = set()

    def op(self, eng, fn, reads=(), writes=(), dma=None, is_out=False):
        o = Op()
        o.eng = eng
        o.fn = fn
        o.signal = False
        o.count = 0
        o.is_dma = dma is not None
        o.semkey = dma
        o.dma_val = 0
        deps = set()
        for r in reads:
            if r.w is not None:
                deps.add(r.w)
        for w in writes:
            if w.w is not None:
                deps.add(w.w)
            for ro in w.rd.values():
                deps.add(ro)
        if o.is_dma:
            prev = self.dma_last.get(dma)
            if prev is not None:
                deps.add(prev)
            self.dma_last[dma] = o
            self.dma_cnt[dma] = self.dma_cnt.get(dma, 0) + 1
            o.dma_val = 16 * self.dma_cnt[dma]
            if is_out:
                self.out_keys.add(dma)
        o.deps = deps
        key = ("dma", dma) if o.is_dma else eng
        for r in reads:
            r.rd[key] = o
        for w in writes:
            w.w = o
            w.rd = {}
        o.idx = len(self.ops[eng])
        self.ops[eng].append(o)
        return o

    def finalize(self):
        for e in self.ENGS:
            for o in self.ops[e]:
                for d in o.deps:
                    if d.is_dma:
                        continue
                    if d.eng == "pe" and o.eng == "pe" and not o.is_dma:
                        continue
                    d.signal = True
        for e in self.ENGS:
            c = 0
            for o in self.ops[e]:
                if o.is_dma:
                    continue
                if o.signal:
                    c += 1
                    o.count = c

    def emit(self, eng_name, eng, eng_sems, dma_sems):
        waited = {}
        for o in self.ops[eng_name]:
            waits = {}
            for d in o.deps:
                if d.is_dma:
                    k = ("d", d.semkey)
                    sem, val = dma_sems[d.semkey], d.dma_val
                else:
                    if d.eng == eng_name and not o.is_dma:
                        if eng_name == "pe":
                            continue
                        if o.idx - d.idx > 3:
                            continue
                    k = ("e", d.eng)
                    sem, val = eng_sems[d.eng], d.count
                if k not in waits or waits[k][1] < val:
                    waits[k] = (sem, val)
            for k, (sem, val) in waits.items():
                if waited.get(k, -1) >= val:
                    continue
                eng.wait_ge(sem, val)
                waited[k] = val
            ins = o.fn(eng)
            if o.is_dma:
                ins.then_inc(dma_sems[o.semkey], 16)
            elif o.signal:
                ins.then_inc(eng_sems[eng_name], 1)


class TT:
    def __init__(self, h, shape):
        self.h = h
        f = 1
        for s in shape[1:]:
            f *= s
        self.F = f

    def ap(self, off, dims, p0=0, pn=128):
        return bass.AP(self.h, p0 * self.F + off, [[self.F, pn]] + [list(d) for d in dims])


def build_program(tiles_override=None, dbg=99):
    nc = bass.Bass("TRN2", target_bir_lowering=False)
    P = Prog()

    def din(name, shape, dtp=F32):
        return nc.dram_tensor(name, shape, dtp, kind="ExternalInput")

    def dout(name, shape):
        return nc.dram_tensor(name, shape, F32, kind="ExternalOutput")

    xp = din("xp", [PB, SEQ, D])
    xs = din("xs", [SB, SL, D])
    sconv = din("sconv", [SB, 2, CD])
    ck = din("ck", [SB, KVR, AD])
    cv = din("cv", [SB, KVR, AD])
    g_mix = din("g_mix", [D])
    w_in = din("w_in", [D, IN_DIM])
    w_conv = din("w_conv", [3, CD])
    w_co = din("w_co", [CD, D])
    rbp = din("rbp", [NH, RBW])
    w_ao = din("w_ao", [AD, D])
    w_out = din("w_out", [D, D])
    g_ffn = din("g_ffn", [D])
    w1 = din("w1", [D, DFF])
    w3 = din("w3", [D, DFF])
    w2 = din("w2", [DFF, D])
    g_fin = din("g_fin", [D])
    yp = dout("yp", [PB, SEQ, D])
    ys = dout("ys", [SB, SL, D])
    convp = dout("convp", [PB, 2, CD])
    kp = dout("kp", [PB, KVR, AD])
    vp = dout("vp", [PB, KVR, AD])
    convs = dout("convs", [SB, 2, CD])
    ks = dout("ks", [SB, KVR, AD])
    vs = dout("vs", [SB, KVR, AD])

    units = []
    units.append(("q", 8, 512, [(w_in, 0, 1536, 512, 0)]))
    units.append(("k", 8, 512, [(w_in, 0, 2048, 512, 0)]))
    units.append(("v", 8, 512, [(w_in, 0, 2560, 512, 0)]))
    for c in range(4):
        units.append((f"conv{c}", 8, 384, [(w_in, 0, 512 + 128 * c, 128, 0),
                                           (w_in, 0, 1024 + 128 * c, 128, 128),
                                           (w_in, 0, 128 * c, 128, 256)]))
    for g in range(4):
        units.append((f"g{g}", 8, 512, [(w_in, 0, 3072 + 512 * g, 512, 0)]))
    units.append(("co", 4, 1024, [(w_co, 0, 0, 1024, 0)]))
    units.append(("ao", 4, 1024, [(w_ao, 0, 0, 1024, 0)]))
    for hf in range(2):
        units.append((f"wo{hf}", 8, 512, [(w_out, 0, 512 * hf, 512, 0)]))
    for i in range(11):
        units.append((f"ff{i}", 8, 512, [(w1, 0, 256 * i, 256, 0), (w3, 0, 256 * i, 256, 256)]))
    KG = [(0, 8), (8, 8), (16, 6)]
    for hf in range(2):
        for kg, (k0, nk) in enumerate(KG):
            units.append((f"w2_{hf}_{kg}", nk, 512, [(w2, 128 * k0, 512 * hf, 512, 0)]))
    NU = len(units)
    UIDX = {u[0]: i for i, u in enumerate(units)}
    src_rowlen = {id(w_in): IN_DIM, id(w_co): D, id(w_ao): D, id(w_out): D, id(w1): DFF, id(w3): DFF, id(w2): D}

    wsc = nc.dram_tensor("wsc", [NU, 128, 4096], BF16, kind="Internal")
    Sdr = nc.dram_tensor("Sdr", [NH, 128, RBW], F32, kind="Internal")

    with ExitStack() as es:
        def sbt(name, shape, dtp):
            return TT(es.enter_context(nc.sbuf_tensor(name, shape, dtp)), shape)

        xbuf = [sbt(f"xbuf{i}", [128, 4, 1024], F32) for i in range(2)]
        xn = [sbt(f"xn{i}", [128, 1024], BF16) for i in range(4)]
        xnT = sbt("xnT", [128, 8, 512], BF16)
        qT = sbt("qT", [128, 4, 2, 512], BF16)
        kT = [sbt(f"kT{i}", [128, 4, 512], BF16) for i in range(3)]
        va = [sbt(f"va{i}", [128, 4, 768], BF16) for i in range(3)]
        ubuf = [sbt(f"u{i}", [128, 528], F32) for i in range(4)]
        zT = sbt("zT", [128, 4, 512], BF16)
        oT = sbt("oT", [128, 4, 512], BF16)
        arena_h = es.enter_context(nc.sbuf_tensor("arena", [128, 8192], F32))
        arena = TT(arena_h, [128, 8192])
        arena_bf = TT(arena_h.bitcast(BF16), [128, 16384])
        biasT = sbt("biasT", [128, 8, 2, 128], F32)
        cbias = sbt("cbias", [128, 8], F32)
        sS = [sbt(f"sS{i}", [128, 2, 128], F32) for i in range(3)]
        pT = [sbt(f"pT{i}", [128, 5, 128], BF16) for i in range(4)]
        merged = sbt("merged", [128, 8, 512], BF16)
        scr = [sbt(f"scr{i}", [128, 512], F32) for i in range(NSCR)]
        wring = [sbt(f"wring{i}", [128, 4096], BF16) for i in range(NS)]
        gfin = sbt("gfin", [128, 1024], F32)
        ident = sbt("ident", [128, 128], BF16)
        gmixc = sbt("gmixc", [128, 8], F32)
        gffnc = sbt("gffnc", [128, 8], F32)
        wcT = sbt("wcT", [128, 3, 4], F32)
        epsc = sbt("epsc", [128, 1], F32)
        onesc = sbt("onesc", [128, 1], F32)
        stats = [[sbt(f"st{n}_{s}", [128, 4], F32) for s in range(4)] for n in range(3)]
        identf = sbt("identf", [128, 128], F32)
        rows = sbt("rows", [16, 520], F32)
        cst = sbt("cst", [128, 4, 8], F32)

        psf = []
        psb = []
        for i in range(8):
            h = es.enter_context(nc.psum_tensor(f"ps{i}", [128, 512], F32))
            psf.append(TT(h, [128, 512]))
            psb.append(TT(h.bitcast(BF16), [128, 1024]))

        R = lambda: Res()
        r_xb = [[R() for _ in range(4)] for _ in range(2)]
        r_xn = [R() for _ in range(4)]
        r_junk = R()
        r_biasHJ = [[R(), R()] for _ in range(NH)]
        r_biasH = r_biasHJ
        r_SdrH = [R() for _ in range(NH)]
        r_rows = R()
        r_sconv = R()
        r_cst = R()
        r_crow = R()
        r_xnT = [R() for _ in range(4)]
        r_qT = [R() for _ in range(4)]
        r_kT = [[R() for _ in range(4)] for _ in range(3)]
        r_va = [[R() for _ in range(4)] for _ in range(3)]
        r_u = [R() for _ in range(4)]
        r_zT = [R() for _ in range(4)]
        r_oT = [[R() for _ in range(4)] for _ in range(4)]
        r_gran = [R() for _ in range(32)]
        r_sga = lambda m: [r_gran[2 * m], r_gran[2 * m + 1]]
        r_sgb = lambda m: [r_gran[16 + 2 * m], r_gran[17 + 2 * m]]
        r_act = lambda c: [r_gran[c]]
        r_sS = [R(), R(), R()]
        r_pT = [R(), R(), R(), R()]
        r_rsum = [R(), R()]
        r_merged = [R() for _ in range(8)]
        r_scr = [R() for _ in range(NSCR)]
        r_wslot = [R() for _ in range(NS)]
        r_wsc = [[] for _ in range(NU)]
        r_const = R()
        r_stats = [[R() for _ in range(4)] for _ in range(3)]
        r_kctok = R()
        r_Sdr = R()
        r_bank = [[R()] for _ in range(8)]
        Sbanks = [(2, 3), (4, 5)]
        r_S = [r_bank[2] + r_bank[3], r_bank[4] + r_bank[5]]
        Oloc = [(6, 0), (7, 0)]
        r_O = [r_bank[6], r_bank[7]]

        state = {"bank": 0, "scr": 0, "xn": 0, "sb": 0, "ob": 0, "cvk": 0, "nb": 8}

        def next_bank():
            b = state["bank"] % state["nb"]
            state["bank"] = (b + 1) % state["nb"]
            return b

        def next_scr():
            s = state["scr"]
            state["scr"] = (s + 1) % NSCR
            return s

        allow = nc.allow_non_contiguous_dma(reason="small one-time / strided loads")
        allow.__enter__()

        P.op("pool", lambda e: e.memset(scr[0].ap(0, [[1, 128]]), 0.0), writes=[r_scr[0]])
        P.op("pool", lambda e: e.affine_select(out=scr[0].ap(0, [[1, 128]]), in_=scr[0].ap(0, [[1, 128]]),
                                               pattern=[[-1, 128]], compare_op=ALU.not_equal, fill=1.0,
                                               base=0, channel_multiplier=1),
             reads=[r_scr[0]], writes=[r_scr[0]])
        P.op("dve", lambda e: e.tensor_copy(out=ident.ap(0, [[1, 128]]), in_=scr[0].ap(0, [[1, 128]])),
             reads=[r_scr[0]], writes=[r_const])
        P.op("dve", lambda e: e.tensor_copy(out=identf.ap(0, [[1, 128]]), in_=scr[0].ap(0, [[1, 128]])),
             reads=[r_scr[0]], writes=[r_const])
        P.op("pool", lambda e: e.memset(epsc.ap(0, [[1, 1]]), EPS), writes=[r_const])
        P.op("pool", lambda e: e.memset(onesc.ap(0, [[1, 1]]), -1.0), writes=[r_const])
        P.op("pool", lambda e: e.memset(qT.ap(0, [[1, 4096]]), 0.0), writes=r_qT)
        for i in range(3):
            P.op("pool", lambda e, i=i: e.memset(va[i].ap(64, [[768, 4], [192, 4], [1, 64]]), 1.0),
                 writes=r_va[i])
        for i in range(2):
            P.op("pool", lambda e, i=i: e.memset(pT[i].ap(64, [[1, 64]], p0=0, pn=64), 0.0), writes=[r_pT[i]])

        P.op("sp", lambda e: e.dma_start(out=rows.ap(0, [[1, 128]], pn=8), in_=bass.AP(g_mix, 0, [[128, 8], [1, 128]])),
             writes=[r_rows], dma="c0")
        P.op("sp", lambda e: e.dma_start(out=rows.ap(128, [[1, 128]], pn=8), in_=bass.AP(g_ffn, 0, [[128, 8], [1, 128]])),
             writes=[r_rows], dma="c1")
        P.op("sp", lambda e: e.dma_start(out=rows.ap(256, [[1, 128]], pn=12), in_=bass.AP(w_conv, 0, [[128, 12], [1, 128]])),
             writes=[r_rows], dma="c2")
        P.op("sp", lambda e: e.dma_start(out=rows.ap(384, [[1, 8]], pn=1), in_=bass.AP(rbp, RBW - 1, [[0, 1], [RBW, 8]])),
             writes=[r_rows], dma="c3")
        P.op("sp", lambda e: e.dma_start(out=gfin.ap(0, [[1, 1024]]), in_=bass.AP(g_fin, 0, [[0, 128], [1, 1024]])),
             writes=[r_const], dma="c0")
        P.op("dve", lambda e: e.memset(rows.ap(392, [[1, 128]], pn=1), 1.0), writes=[r_rows])
        bpro = 0
        P.op("pe", lambda e: e.transpose(out=psf[bpro].ap(0, [[1, 8]]), in_=rows.ap(0, [[1, 128]], pn=8),
                                         identity=identf.ap(0, [[1, 8]], pn=8)),
             reads=[r_rows, r_const], writes=r_bank[bpro])
        P.op("pe", lambda e: e.transpose(out=psf[bpro].ap(8, [[1, 8]]), in_=rows.ap(128, [[1, 128]], pn=8),
                                         identity=identf.ap(0, [[1, 8]], pn=8)),
             reads=[r_rows, r_const], writes=r_bank[bpro])
        P.op("pe", lambda e: e.transpose(out=psf[bpro].ap(16, [[1, 12]]), in_=rows.ap(256, [[1, 128]], pn=12),
                                         identity=identf.ap(0, [[1, 12]], pn=12)),
             reads=[r_rows, r_const], writes=r_bank[bpro])
        P.op("pe", lambda e: e.matmul(psf[bpro].ap(32, [[1, 8]]), lhsT=rows.ap(392, [[1, 128]], pn=1),
                                      rhs=rows.ap(384, [[1, 8]], pn=1), start=True, stop=True),
             reads=[r_rows], writes=r_bank[bpro])
        P.op("dve", lambda e: e.tensor_copy(out=gmixc.ap(0, [[1, 8]]), in_=psf[bpro].ap(0, [[1, 8]])),
             reads=r_bank[bpro], writes=[r_const])
        P.op("dve", lambda e: e.tensor_copy(out=gffnc.ap(0, [[1, 8]]), in_=psf[bpro].ap(8, [[1, 8]])),
             reads=r_bank[bpro], writes=[r_const])
        P.op("dve", lambda e: e.tensor_copy(out=wcT.ap(0, [[1, 12]]), in_=psf[bpro].ap(16, [[1, 12]])),
             reads=r_bank[bpro], writes=[r_const])
        P.op("dve", lambda e: e.tensor_copy(out=cbias.ap(0, [[1, 8]]), in_=psf[bpro].ap(32, [[1, 8]])),
             reads=r_bank[bpro], writes=[r_const])
        state["bank"] = 1

        def setup_bias():
            for h in range(NH):
                P.op("sp", lambda e, h=h: e.dma_start(out=bass.AP(Sdr, h * 128 * RBW, [[RBW, 128], [1, RBW]]),
                                                      in_=bass.AP(rbp, h * RBW, [[0, 128], [1, RBW]])),
                     writes=[r_SdrH[h]], dma=f"sb{h}")
            for h in range(NH):
                for jj, j in enumerate((4, 3)):
                    off = h * 128 * RBW + 128 * (4 - j) + 127
                    P.op("sp", lambda e, h=h, jj=jj, off=off: e.dma_start(
                        out=biasT.ap((h * 2 + jj) * 128, [[1, 128]]),
                        in_=bass.AP(Sdr, off, [[RBW - 1, 128], [1, 128]])),
                        reads=[r_SdrH[h]], writes=[r_biasHJ[h][jj]], dma=f"sb{h}_{jj}")

        cvi = 0
        for ui, (uname, nk, ncols, pieces) in enumerate(units):
            for (src, row0, col0, npc, dcol0) in pieces:
                rl = src_rowlen[id(src)]
                nsplit = 8 if ui < 3 else (2 if ui < 7 else 1)
                kstep = nk // nsplit
                for sp_ in range(nsplit):
                    k0_ = sp_ * kstep
                    src_ap = bass.AP(src, (row0 + 128 * k0_) * rl + col0, [[rl, 128], [128 * rl, kstep], [1, npc]])
                    dst_ap = bass.AP(wsc, ui * 128 * 4096 + k0_ * ncols + dcol0, [[4096, 128], [ncols, kstep], [1, npc]])
                    pr = Res()
                    r_wsc[ui].append(pr)
                    P.op("pool", lambda e, s=src_ap, d=dst_ap: e.dma_start(out=d, in_=s),
                         writes=[pr], dma=f"cv{cvi % 16}")
                    cvi += 1

        for (dst, src, key) in ((ks, ck, "pk"), (vs, cv, "pv")):
            P.op("act", lambda e, dst=dst, src=src: e.dma_start(
                out=bass.AP(dst, 0, [[KVR * AD, SB], [1, (KVR - SL) * AD]]),
                in_=bass.AP(src, SL * AD, [[KVR * AD, SB], [1, (KVR - SL) * AD]])),
                dma=key, is_out=True)

        tiles = [("p", s, ti) for s in range(PB) for ti in range(4)] + [("s", 0, 0)]
        if tiles_override is not None:
            tiles = tiles_override
        NT = len(tiles)
        wstate = {"issued": 0, "cur": 0}
        total_units = NT * NU

        def issue_load():
            n = wstate["issued"]
            if n >= total_units:
                return
            ui = n % NU
            slot = n % NS
            nk, ncols = units[ui][1], units[ui][2]
            sz = nk * ncols
            P.op("sp", lambda e, slot=slot, ui=ui, sz=sz: e.dma_start(
                out=wring[slot].ap(0, [[1, sz]]), in_=bass.AP(wsc, ui * 128 * 4096, [[4096, 128], [1, sz]])),
                reads=r_wsc[ui], writes=[r_wslot[slot]], dma=f"w{slot}")
            wstate["issued"] = n + 1

        class WUnit:
            def __init__(self, n):
                self.slot = n % NS
                ui = n % NU
                self.name = units[ui][0]
                self.nk, self.ncols = units[ui][1], units[ui][2]
                self.res = r_wslot[self.slot]

            def ap(self, kc, c0, n):
                return wring[self.slot].ap(kc * self.ncols + c0, [[1, n]])

        def take_unit(name):
            n = wstate["cur"]
            wu = WUnit(n)
            assert wu.name == name, (wu.name, name)
            wstate["cur"] = n + 1
            return wu

        def done_unit():
            issue_load()

        def tile_geom(t):
            kind, seq, ti = tiles[t]
            if kind == "p":
                return dict(kind=kind, seq=seq, ti=ti, SUB=128, TT=512, nseg=1, L=512, par=t % 2)
            return dict(kind=kind, seq=0, ti=0, SUB=64, TT=256, nseg=4, L=64, par=t % 2)

        def load_x(t):
            g = tile_geom(t)
            par = g["par"]
            if g["kind"] == "p":
                src = bass.AP(xp, (g["seq"] * SEQ + g["ti"] * 512) * D, [[D, 128], [128 * D, 4], [1, D]])
                dstap = xbuf[par].ap(0, [[1024, 4], [1, 1024]])
            else:
                src = bass.AP(xs, 0, [[D, 64], [SL * D, 4], [1, D]])
                dstap = xbuf[par].ap(0, [[1024, 4], [1, 1024]], pn=64)
            P.op("sp", lambda e: e.dma_start(out=dstap, in_=src), writes=r_xb[par], dma=f"x{par}")

        def rms_stats(g, norm, sub, jt=None, jres=None):
            SUB, par = g["SUB"], g["par"]
            st = stats[norm][sub]
            rst = r_stats[norm][sub]
            xap = xbuf[par].ap(sub * 1024, [[1, 1024]], pn=SUB)
            P.op("act", lambda e: e.activation(out=jt.ap(0, [[1, 1024]], pn=SUB), in_=xap, func=AF.Square,
                                               accum_out=st.ap(0, [[1, 1]], pn=SUB)),
                 reads=[r_xb[par][sub], rst], writes=jres + [rst])
            P.op("act", lambda e: e.activation(out=st.ap(1, [[1, 1]], pn=SUB), in_=st.ap(0, [[1, 1]], pn=SUB),
                                               func=AF.Ln, scale=1.0 / D, bias=epsc.ap(0, [[1, 1]], pn=SUB)),
                 reads=[rst, r_const], writes=[rst])
            P.op("act", lambda e: e.activation(out=st.ap(2, [[1, 1]], pn=SUB), in_=st.ap(1, [[1, 1]], pn=SUB),
                                               func=AF.Exp, scale=-0.5),
                 reads=[rst], writes=[rst])
            return xap, st.ap(2, [[1, 1]], pn=SUB), rst

        def rms_front(g, norm, sub):
            SUB, par = g["SUB"], g["par"]
            xap, rstd, rst = rms_stats(g, norm, sub, jt=xn[sub], jres=[r_xn[sub]])
            P.op("act", lambda e: e.activation(out=xn[sub].ap(0, [[1, 1024]], pn=SUB), in_=xap, func=AF.Copy,
                                               scale=rstd),
                 reads=[r_xb[par][sub], rst], writes=[r_xn[sub]])

        def rms_back(g, sub, gcol):
            SUB = g["SUB"]
            b = next_bank()
            for kc in range(8):
                P.op("pe", lambda e, kc=kc: e.transpose(out=psb[b].ap(kc * SUB, [[1, SUB]]),
                                                        in_=xn[sub].ap(kc * 128, [[1, 128]], pn=SUB),
                                                        identity=ident.ap(0, [[1, SUB]], pn=SUB)),
                     reads=[r_xn[sub], r_const], writes=r_bank[b])
            P.op("dve", lambda e: e.tensor_tensor(out=xnT.ap(sub * SUB, [[512, 8], [1, SUB]]),
                                                  in0=psb[b].ap(0, [[SUB, 8], [1, SUB]]),
                                                  in1=gcol.ap(0, [[1, 8], [0, SUB]]), op=ALU.mult),
                 reads=r_bank[b] + [r_const], writes=[r_xnT[sub]])

        def rms_to_T(g, norm, sub, gcol):
            rms_front(g, norm, sub)
            rms_back(g, sub, gcol)

        def dense_fm(wu, c0, TTk, rhs_t, rhs_res, nk):
            b = next_bank()
            for kc in range(nk):
                P.op("pe", lambda e, kc=kc: e.matmul(psf[b].ap(0, [[1, TTk]]), lhsT=wu.ap(kc, c0, 128),
                                                     rhs=rhs_t.ap(kc * 512, [[1, TTk]]),
                                                     start=(kc == 0), stop=(kc == nk - 1)),
                     reads=[wu.res] + rhs_res, writes=r_bank[b])
            return b

        def dense_tm(wu, g, sub, lhs_t, lhs_res, kcs, bank=None, first=True, last=True, kc_base=0):
            SUB = g["SUB"]
            b = next_bank() if bank is None else bank
            n = len(kcs)
            for i, kc in enumerate(kcs):
                P.op("pe", lambda e, i=i, kc=kc: e.matmul(
                    psf[b].ap(0, [[1, 512]], pn=SUB),
                    lhsT=lhs_t.ap((kc_base + kc) * 512 + sub * SUB, [[1, SUB]]),
                    rhs=wu.ap(kc, 0, 512),
                    start=(first and i == 0), stop=(last and i == n - 1)),
                    reads=[wu.res] + lhs_res, writes=r_bank[b])
            return b

        def attn_unit_qk(h, NQ, qcol0, blocks):
            sb = state["sb"]
            state["sb"] = 1 - sb
            hp, hc = h % 2, h // 2
            for (j, kt, kcol0, nk, vt, voff, vres, kres) in blocks:
                bank = Sbanks[sb][0] if j < 3 else Sbanks[sb][1]
                col = j * 128 if j < 3 else (j - 3) * 128
                P.op("pe", lambda e, bank=bank, col=col, kt=kt, kcol0=kcol0, nk=nk: e.matmul(
                    psf[bank].ap(col, [[1, NQ]], pn=nk),
                    lhsT=kt.ap(hc * 512 + kcol0, [[1, nk]]),
                    rhs=qT.ap((hc * 2 + hp) * 512 + qcol0, [[1, NQ]]),
                    start=True, stop=True),
                    reads=[kres, r_qT[hc]], writes=r_S[sb])
            return sb

        def attn_unit_softmax(h, NQ, blocks, sb):
            pb = sb
            js = [b[0] for b in blocks]
            nks = {b[0]: b[3] for b in blocks}
            bankA, bankB = Sbanks[sb]
            low = [j for j in js if j < 3]
            if low:
                j0, nj = low[0], len(low)
                P.op("act", lambda e, j0=j0, nj=nj: e.activation(out=pT[pb].ap(j0 * 128, [[128, nj], [1, NQ]]),
                                                                 in_=psf[bankA].ap(j0 * 128, [[128, nj], [1, NQ]]),
                                                                 func=AF.Exp, bias=cbias.ap(h, [[1, 1]])),
                     reads=r_S[sb] + [r_const], writes=[r_pT[pb]])
            high = [j for j in js if j >= 3]
            if not high:
                return
            groups = [(j, 1, nks[j]) for j in high]
            for (j0, nj, nk) in groups:
                jj = j0 - 3
                assert nj == 1
                P.op("dve", lambda e, jj=jj, nj=nj, nk=nk, j0=j0: e.tensor_tensor(
                    out=sS[sb].ap(jj * 128, [[128, nj], [1, NQ]], pn=nk),
                    in0=psf[bankB].ap(jj * 128, [[128, nj], [1, NQ]], pn=nk),
                    in1=biasT.ap((h * 2 + (4 - j0)) * 128, [[128, nj], [1, NQ]], pn=nk), op=ALU.add),
                    reads=r_S[sb] + r_biasHJ[h], writes=[r_sS[sb]])
            for (j0, nj, nk) in groups:
                jj = j0 - 3
                P.op("act", lambda e, j0=j0, jj=jj, nj=nj, nk=nk: e.activation(
                    out=pT[pb].ap(j0 * 128, [[128, nj], [1, NQ]], pn=nk),
                    in_=sS[sb].ap(jj * 128, [[128, nj], [1, NQ]], pn=nk), func=AF.Exp,
                    bias=cbias.ap(h, [[1, 1]], pn=nk)),
                    reads=[r_sS[sb], r_const], writes=[r_pT[pb]])

        def attn_unit_pv(h, NQ, blocks, sb):
            pb = sb
            ob = state["ob"]
            state["ob"] = (ob + 1) % 2
            obank, ocol = Oloc[ob]
            hp, hc = h % 2, h // 2
            mms = []
            for (j, kt, kcol0, nk, vt, voff, vres, kres) in blocks:
                if j == 0 and NQ == 128:
                    mms.append((j, 0, 128, 0, 64, vt, voff, vres))
                    mms.append((j, 64, 64, 64, 64, vt, voff, vres))
                else:
                    mms.append((j, 0, nk, 0, NQ, vt, voff, vres))
            n = len(mms)
            for i, (j, kp0, kn, q0, qn, vt, voff, vres) in enumerate(mms):
                P.op("pe", lambda e, i=i, j=j, kp0=kp0, kn=kn, q0=q0, qn=qn, vt=vt, voff=voff: e.matmul(
                    psf[obank].ap(ocol + q0, [[1, qn]]),
                    lhsT=vt.ap(voff + hc * 192 + hp * 64, [[1, 128]], p0=kp0, pn=kn),
                    rhs=pT[pb].ap(j * 128 + q0, [[1, qn]], p0=kp0, pn=kn),
                    start=(i == 0), stop=(i == n - 1), skip_group_check=True),
                    reads=[vres, r_pT[pb]], writes=r_O[ob])
            return ob

        def attn_unit_norm(h, NQ, ob, ocol0, osub):
            obank, ocol = Oloc[ob]
            hp, hc = h % 2, h // 2
            ri = next_scr()
            srow = 64 if hp == 0 else 0
            drow = 0 if hp == 0 else 64
            P.op("dve", lambda e: e.reciprocal(out=scr[ri].ap(0, [[1, NQ]], p0=drow, pn=64),
                                               in_=psf[obank].ap(ocol, [[1, NQ]], p0=srow, pn=64)),
                 reads=r_O[ob], writes=[r_scr[ri]])
            P.op("dve", lambda e: e.tensor_tensor(out=oT.ap(hc * 512 + ocol0, [[1, NQ]], p0=drow, pn=64),
                                                  in0=psf[obank].ap(ocol, [[1, NQ]], p0=drow, pn=64),
                                                  in1=scr[ri].ap(0, [[1, NQ]], p0=drow, pn=64), op=ALU.mult),
                 reads=r_O[ob] + [r_scr[ri]], writes=[r_oT[hc][osub]])

        def run_attention(unit_list, hooks=None):
            unit_list = unit_list[:DBG_ATTN]
            n = len(unit_list)
            sbs = [None] * n
            obs = [None] * n
            if DBG_PIPE == 0:
                for i in range(n):
                    if hooks and i in hooks:
                        hooks[i]()
                    u = unit_list[i]
                    sbs[i] = attn_unit_qk(u["h"], u["NQ"], u["qcol0"], u["blocks"])
                    attn_unit_softmax(u["h"], u["NQ"], u["blocks"], sbs[i])
                    obs[i] = attn_unit_pv(u["h"], u["NQ"], u["blocks"], sbs[i])
                    attn_unit_norm(u["h"], u["NQ"], obs[i], u["ocol0"], u["osub"])
                return
            if DBG_PIPE == 2:
                for i in range(n + 1):
                    if hooks and i in hooks:
                        hooks[i]()
                    if i < n:
                        u = unit_list[i]
                        sbs[i] = attn_unit_qk(u["h"], u["NQ"], u["qcol0"], u["blocks"])
                        attn_unit_softmax(u["h"], u["NQ"], u["blocks"], sbs[i])
                    if i >= 1:
                        u = unit_list[i - 1]
                        obs[i - 1] = attn_unit_pv(u["h"], u["NQ"], u["blocks"], sbs[i - 1])
                        attn_unit_norm(u["h"], u["NQ"], obs[i - 1], u["ocol0"], u["osub"])
                return
            for i in range(n + 2):
                if hooks and i in hooks:
                    hooks[i]()
                if i < n:
                    u = unit_list[i]
                    sbs[i] = attn_unit_qk(u["h"], u["NQ"], u["qcol0"], u["blocks"])
                    attn_unit_softmax(u["h"], u["NQ"], u["blocks"], sbs[i])
                if 0 <= i - 1 < n:
                    u = unit_list[i - 1]
                    obs[i - 1] = attn_unit_pv(u["h"], u["NQ"], u["blocks"], sbs[i - 1])
                if 0 <= i - 2 < n:
                    u = unit_list[i - 2]
                    attn_unit_norm(u["h"], u["NQ"], obs[i - 2], u["ocol0"], u["osub"])

        def kv_store(g, sub, which, sidx):
            SUB = g["SUB"]
            if g["kind"] == "p":
                dst = kp if which == "k" else vp
                d_ap = bass.AP(dst, (g["seq"] * KVR + sub * 128) * AD, [[AD, 128], [1, AD]])
            else:
                dst = ks if which == "k" else vs
                d_ap = bass.AP(dst, (sub * KVR + (KVR - SL)) * AD, [[AD, 64], [1, AD]])
            P.op("pool", lambda e: e.dma_start(out=d_ap, in_=scr[sidx].ap(0, [[1, 512]], pn=SUB)),
                 reads=[r_scr[sidx]], dma=f"kv{sidx}", is_out=True)

        def proj_phase(t):
            g = tile_geom(t)
            SUB, TTk, par = g["SUB"], g["TT"], g["par"]
            prompt = g["kind"] == "p"
            cur = (g["ti"] % 2) if prompt else 2
            emit_kv = (not prompt) or g["ti"] == 3
            xres = list(r_xnT)
            wu = take_unit("q")
            for m in range(4):
                b = dense_fm(wu, m * 128, TTk, xnT, xres, 8)
                for hp_ in range(2):
                    P.op("act", lambda e, m=m, b=b, hp_=hp_: e.activation(
                        out=qT.ap((m * 2 + hp_) * 512, [[1, TTk]], p0=hp_ * 64, pn=64),
                        in_=psf[b].ap(0, [[1, TTk]], p0=hp_ * 64, pn=64), func=AF.Copy, scale=0.125),
                        reads=r_bank[b], writes=[r_qT[m]])
            done_unit()
            wu = take_unit("k")
            for m in range(4):
                b = dense_fm(wu, m * 128, TTk, xnT, xres, 8)
                P.op("dve", lambda e, m=m, b=b: e.tensor_copy(out=kT[cur].ap(m * 512, [[1, TTk]]),
                                                              in_=psf[b].ap(0, [[1, TTk]])),
                     reads=r_bank[b], writes=[r_kT[cur][m]])
            if emit_kv:
                for sub in range(4):
                    b = dense_tm(wu, g, sub, xnT, xres, list(range(8)))
                    si = next_scr()
                    P.op("act", lambda e, b=b, si=si: e.activation(out=scr[si].ap(0, [[1, 512]], pn=SUB),
                                                                   in_=psf[b].ap(0, [[1, 512]], pn=SUB), func=AF.Copy),
                         reads=r_bank[b], writes=[r_scr[si]])
                    kv_store(g, sub, "k", si)
            done_unit()
            wu = take_unit("v")
            for sub in range(4):
                b = dense_tm(wu, g, sub, xnT, xres, list(range(8)))
                for par_h in range(2):
                    P.op("dve", lambda e, b=b, sub=sub, par_h=par_h: e.tensor_copy(
                        out=va[cur].ap(sub * 768 + par_h * 128, [[192, 4], [1, 64]], pn=SUB),
                        in_=psf[b].ap(par_h * 64, [[128, 4], [1, 64]], pn=SUB)),
                        reads=r_bank[b], writes=[r_va[cur][sub]])
                if emit_kv:
                    si = next_scr()
                    P.op("dve", lambda e, b=b, si=si: e.tensor_copy(out=scr[si].ap(0, [[1, 512]], pn=SUB),
                                                                    in_=psf[b].ap(0, [[1, 512]], pn=SUB)),
                         reads=r_bank[b], writes=[r_scr[si]])
                    kv_store(g, sub, "v", si)
            done_unit()
            nseg, L = g["nseg"], g["L"]
            W2 = L + 2
            for c in range(4):
                wu = take_unit(f"conv{c}")
                b1 = dense_fm(wu, 0, TTk, xnT, xres, 8)
                b2 = dense_fm(wu, 128, TTk, xnT, xres, 8)
                b3 = dense_fm(wu, 256, TTk, xnT, xres, 8)
                done_unit()
                s1 = next_scr()
                P.op("act", lambda e, b1=b1, s1=s1: e.activation(out=scr[s1].ap(0, [[1, TTk]]),
                                                                 in_=psf[b1].ap(0, [[1, TTk]]), func=AF.Copy),
                     reads=r_bank[b1], writes=[r_scr[s1]])
                if prompt and g["ti"] == 0:
                    P.op("dve", lambda e, c=c: e.memset(ubuf[c].ap(0, [[1, 2]]), 0.0), writes=[r_u[c]])
                elif prompt:
                    P.op("dve", lambda e, c=c: e.tensor_copy(out=ubuf[c].ap(0, [[1, 2]]), in_=ubuf[c].ap(512, [[1, 2]])),
                         reads=[r_u[c]], writes=[r_u[c]])
                else:
                    if c == 0:
                        ssc = next_scr()
                        P.op("sp", lambda e, ssc=ssc: e.dma_start(out=scr[ssc].ap(0, [[1, 512]], pn=8),
                                                                  in_=bass.AP(sconv, 0, [[CD, 8], [1, CD]])),
                             writes=[r_scr[ssc]], dma="c3")
                        bsc = next_bank()
                        state["bsc"] = bsc
                        for c2 in range(4):
                            P.op("pe", lambda e, c2=c2, bsc=bsc, ssc=ssc: e.transpose(
                                out=psf[bsc].ap(c2 * 8, [[1, 8]]), in_=scr[ssc].ap(c2 * 128, [[1, 128]], pn=8),
                                identity=identf.ap(0, [[1, 8]], pn=8)),
                                reads=[r_scr[ssc], r_const], writes=r_bank[bsc])
                        for c2 in range(4):
                            P.op("dve", lambda e, c2=c2, bsc=bsc: e.tensor_copy(
                                out=ubuf[c2].ap(0, [[W2, 4], [1, 2]]), in_=psf[bsc].ap(c2 * 8, [[2, 4], [1, 2]])),
                                reads=r_bank[bsc], writes=[r_u[c2]])
                P.op("dve", lambda e, c=c, b2=b2, s1=s1: e.tensor_tensor(
                    out=ubuf[c].ap(2, [[W2, nseg], [1, L]]), in0=psf[b2].ap(0, [[L, nseg], [1, L]]),
                    in1=scr[s1].ap(0, [[L, nseg], [1, L]]), op=ALU.mult),
                    reads=r_bank[b2] + [r_scr[s1]], writes=[r_u[c]])
                s2 = next_scr()
                P.op("dve", lambda e, c=c, s2=s2: e.tensor_scalar(
                    out=scr[s2].ap(0, [[L, nseg], [1, L]]), in0=ubuf[c].ap(0, [[W2, nseg], [1, L]]),
                    scalar1=wcT.ap(0 * 4 + c, [[1, 1]]), scalar2=None, op0=ALU.mult),
                    reads=[r_u[c], r_const], writes=[r_scr[s2]])
                for tap in (1, 2):
                    P.op("dve", lambda e, c=c, s2=s2, tap=tap: e.scalar_tensor_tensor(
                        out=scr[s2].ap(0, [[L, nseg], [1, L]]), in0=ubuf[c].ap(tap, [[W2, nseg], [1, L]]),
                        scalar=wcT.ap(tap * 4 + c, [[1, 1]]), in1=scr[s2].ap(0, [[L, nseg], [1, L]]),
                        op0=ALU.mult, op1=ALU.add),
                        reads=[r_u[c], r_const, r_scr[s2]], writes=[r_scr[s2]])
                P.op("dve", lambda e, c=c, b3=b3, s2=s2: e.tensor_tensor(
                    out=zT.ap(c * 512, [[1, TTk]]), in0=psf[b3].ap(0, [[1, TTk]]),
                    in1=scr[s2].ap(0, [[1, TTk]]), op=ALU.mult),
                    reads=r_bank[b3] + [r_scr[s2]], writes=[r_zT[c]])
                if (prompt and g["ti"] == 3) or not prompt:
                    ncs = 2 if prompt else 8
                    if prompt:
                        P.op("dve", lambda e, c=c: e.tensor_copy(out=cst.ap(c * 8, [[1, 2]]), in_=ubuf[c].ap(512, [[1, 2]])),
                             reads=[r_u[c]], writes=[r_cst])
                    else:
                        P.op("dve", lambda e, c=c: e.tensor_copy(out=cst.ap(c * 8, [[2, 4], [1, 2]]),
                                                                 in_=ubuf[c].ap(L, [[W2, 4], [1, 2]])),
                             reads=[r_u[c]], writes=[r_cst])
                    if c == 3:
                        bcs = next_bank()
                        for c2 in range(4):
                            P.op("pe", lambda e, c2=c2, bcs=bcs, ncs=ncs: e.transpose(
                                out=psf[bcs].ap(c2 * 128, [[1, 128]], pn=ncs), in_=cst.ap(c2 * 8, [[1, ncs]]),
                                identity=identf.ap(0, [[1, 128]])),
                                reads=[r_cst, r_const], writes=r_bank[bcs])
                        sic = next_scr()
                        P.op("dve", lambda e, bcs=bcs, ncs=ncs, sic=sic: e.tensor_copy(
                            out=scr[sic].ap(0, [[1, 512]], pn=ncs), in_=psf[bcs].ap(0, [[1, 512]], pn=ncs)),
                            reads=r_bank[bcs], writes=[r_scr[sic]])
                        if prompt:
                            d_ap = bass.AP(convp, g["seq"] * 2 * CD, [[CD, 2], [1, CD]])
                        else:
                            d_ap = bass.AP(convs, 0, [[CD, 8], [1, CD]])
                        P.op("pool", lambda e, d_ap=d_ap, ncs=ncs, sic=sic: e.dma_start(
                            out=d_ap, in_=scr[sic].ap(0, [[1, 512]], pn=ncs)),
                            reads=[r_scr[sic]], dma="cs", is_out=True)
            for gi in range(4):
                wu = take_unit(f"g{gi}")
                for mm_ in range(4):
                    b = dense_fm(wu, mm_ * 128, TTk, xnT, xres, 8)
                    m = (gi % 2) * 4 + mm_
                    if gi < 2:
                        off, rr = m * 512, r_sga(m)
                    else:
                        off, rr = 4096 + m * 512, r_sgb(m)
                    P.op("act", lambda e, b=b, off=off: e.activation(out=arena.ap(off, [[1, TTk]]),
                                                                     in_=psf[b].ap(0, [[1, TTk]]), func=AF.Sigmoid),
                         reads=r_bank[b], writes=rr)
                done_unit()

        def attention_prompt(t):
            g = tile_geom(t)
            ti = g["ti"]
            cur, prev = ti % 2, (ti - 1) % 2
            qb0 = ti * 4
            kblocks = []
            for kbg in range(max(0, qb0 - 4), qb0 + 4):
                s_lo = max(kbg, qb0) - qb0
                s_hi = min(kbg + 4, qb0 + 3) - qb0
                st_ = cur if kbg >= qb0 else prev
                subs = list(range(s_lo, s_hi + 1))
                js = [kbg - (qb0 + sb_) + 4 for sb_ in subs]
                nh = sum(1 for j in js if j >= 3)
                kblocks.append(dict(kbg=kbg, set=st_, kb=kbg % 4, s_lo=s_lo, n=len(subs), nh=nh, js=js))
            Sb, Ob = (2, 3, 4, 5), (6, 7)
            LOOK = 3
            steps = [(h, kbi) for h in range(NH) for kbi in range(len(kblocks))]
            nsteps = len(steps)
            info = [None] * nsteps
            cnt = {"H": 0, "L": 0, "p": 0, "s": 0}

            def do_qk_softmax(i):
                h, kbi = steps[i]
                kbk = kblocks[kbi]
                hp, hc = h % 2, h // 2
                st_, kb, s_lo, n, nh = kbk["set"], kbk["kb"], kbk["s_lo"], kbk["n"], kbk["nh"]
                pb = cnt["p"] % 4
                cnt["p"] += 1
                sbk = Sb[cnt["H"] % 4]
                cnt["H"] += 1
                lhs = kT[st_].ap(hc * 512 + kb * 128, [[1, 128]])
                kres = r_kT[st_][hc]
                P.op("pe", lambda e: e.matmul(psf[sbk].ap(0, [[1, n * 128]]), lhsT=lhs,
                                              rhs=qT.ap((hc * 2 + hp) * 512 + s_lo * 128, [[1, n * 128]]),
                                              start=True, stop=True),
                     reads=[kres, r_qT[hc]], writes=r_bank[sbk])
                if nh:
                    jfirst = kbk["js"][0]
                    boff = (h * 2 + (4 - jfirst)) * 128
                    P.op("dve", lambda e: e.tensor_tensor(out=psf[sbk].ap(0, [[1, nh * 128]]),
                                                          in0=psf[sbk].ap(0, [[1, nh * 128]]),
                                                          in1=biasT.ap(boff, [[1, nh * 128]]), op=ALU.add),
                         reads=r_bank[sbk] + r_biasHJ[h], writes=r_bank[sbk])
                P.op("act", lambda e: e.activation(out=pT[pb].ap(0, [[1, n * 128]]),
                                                   in_=psf[sbk].ap(0, [[1, n * 128]]), func=AF.Exp,
                                                   bias=cbias.ap(h, [[1, 1]])),
                     reads=r_bank[sbk] + [r_const], writes=[r_pT[pb]])
                if kbk["js"][-1] == 0:
                    P.op("pool", lambda e: e.memset(pT[pb].ap(n * 128 - 64, [[1, 64]], p0=0, pn=64), 0.0),
                         reads=[r_pT[pb]], writes=[r_pT[pb]])
                info[i] = pb

            pend = []
            cur_i = [0]
            NDEFER = 3

            def run_pend(upto, bank=None):
                k = 0
                while k < len(pend):
                    if pend[k][0] <= upto or (bank is not None and pend[k][1] == bank):
                        pend.pop(k)[2]()
                    else:
                        k += 1

            def do_pv(i):
                h, kbi = steps[i]
                kbk = kblocks[kbi]
                if kbi == 0:
                    run_pend(-1, bank=Ob[h % 2])
                hp, hc = h % 2, h // 2
                st_, kb, s_lo, n = kbk["set"], kbk["kb"], kbk["s_lo"], kbk["n"]
                pb = info[i]
                ob = Ob[h % 2]
                vres = r_va[st_][kb]
                voff = kb * 768 + hc * 192 + hp * 64
                first, last = (kbi == 0), (kbi == len(kblocks) - 1)
                q0 = s_lo * 128
                mms = [(0, 128, 0, n * 128)]
                for mi, (kp0, kn, c0, cn) in enumerate(mms):
                    P.op("pe", lambda e, kp0=kp0, kn=kn, c0=c0, cn=cn, mi=mi: e.matmul(
                        psf[ob].ap(q0 + c0, [[1, cn]]),
                        lhsT=va[st_].ap(voff, [[1, 128]], p0=kp0, pn=kn),
                        rhs=pT[pb].ap(c0, [[1, cn]], p0=kp0, pn=kn),
                        start=(first and mi == 0), stop=(last and mi == len(mms) - 1), skip_group_check=True),
                        reads=[vres, r_pT[pb]], writes=r_bank[ob])
                if last:
                    srow = 64 if hp == 0 else 0
                    drow = 0 if hp == 0 else 64

                    def norm():
                        ri = next_scr()
                        rs = scr[ri]
                        P.op("act", lambda e: e.activation(out=rs.ap(0, [[1, 512]], p0=drow, pn=64),
                                                           in_=psf[ob].ap(0, [[1, 512]], p0=srow, pn=64), func=AF.Ln),
                             reads=r_bank[ob], writes=[r_scr[ri]])
                        P.op("act", lambda e: e.activation(out=rs.ap(0, [[1, 512]], p0=drow, pn=64),
                                                           in_=rs.ap(0, [[1, 512]], p0=drow, pn=64), func=AF.Exp,
                                                           scale=-1.0),
                             reads=[r_scr[ri]], writes=[r_scr[ri]])
                        P.op("dve", lambda e: e.tensor_tensor(out=oT.ap(hc * 512, [[1, 512]], p0=drow, pn=64),
                                                              in0=psf[ob].ap(0, [[1, 512]], p0=drow, pn=64),
                                                              in1=rs.ap(0, [[1, 512]], p0=drow, pn=64), op=ALU.mult),
                             reads=r_bank[ob] + [r_scr[ri]], writes=r_oT[hc])
                    pend.append([cur_i[0] + NDEFER, ob, norm])

            state["nb"] = 2
            for i in range(nsteps + LOOK):
                cur_i[0] = i
                if i < nsteps:
                    do_qk_softmax(i)
                run_pend(i)
                if i >= LOOK:
                    do_pv(i - LOOK)
            run_pend(1 << 30)
            state["nb"] = 8

        def attention_phase(t):
            g = tile_geom(t)
            ul = []
            state["nb"] = 2
            if g["kind"] == "p":
                ti = g["ti"]
                cur, prev = ti % 2, (ti - 1) % 2
                for sub in range(4):
                    qbg = ti * 4 + sub
                    blocks = []
                    for j in range(5):
                        kbg = qbg - 4 + j
                        if kbg < 0 or j < DBG_JMIN:
                            continue
                        st_ = cur if kbg // 4 == ti else prev
                        kb = kbg % 4
                        blocks.append((j, kT[st_], kb * 128, 128, va[st_], kb * 768, r_va[st_][kb], None))
                    for h in range(NH):
                        bl = [(j, kt, kc0, nk, vt, voff, vres, r_kT[kt_i(kt)][h // 2]) for (j, kt, kc0, nk, vt, voff, vres, _) in blocks]
                        ul.append(dict(h=h, NQ=128, qcol0=sub * 128, blocks=bl, ocol0=sub * 128, osub=sub))
                run_attention(ul)
            else:
                if not state.get("cache_early_done"):
                    cache_dma(0)
                cache_proc(0, 0)
                cache_dma(1)
                for s in range(SB):
                    cs = s % 2
                    blocks = []
                    for j in range(4):
                        blocks.append((j, kT[cs], j * 128, 128, va[cs], j * 768, r_va[cs][j], None))
                    blocks.append((4, kT[2], s * 64, 64, va[2], s * 768, r_va[2][s], None))
                    for h in range(NH):
                        bl = [(j, kt, kc0, nk, vt, voff, vres, r_kT[kt_i(kt)][h // 2]) for (j, kt, kc0, nk, vt, voff, vres, _) in blocks]
                        ul.append(dict(h=h, NQ=64, qcol0=s * 64, blocks=bl, ocol0=s * 64, osub=s))
                run_attention(ul, hooks={5: (lambda: (cache_proc(1, 1), cache_dma(2))),
                                         11: (lambda: (cache_proc(2, 0), cache_dma(3))),
                                         18: (lambda: cache_proc(3, 1))})
            state["nb"] = 8

        def kt_i(kt):
            for i in range(3):
                if kT[i] is kt:
                    return i
            raise AssertionError

        merged_f = TT(merged.h.bitcast(F32), [128, 2048])
        xn_f = [TT(xn[i].h.bitcast(F32), [128, 512]) for i in range(4)]

        def cache_dma(s):
            P.op("sp", lambda e: e.dma_start(out=merged_f.ap(0, [[512, 4], [1, 512]]),
                                             in_=bass.AP(ck, s * KVR * AD, [[AD, 128], [128 * AD, 4], [1, AD]])),
                 writes=list(r_merged), dma="ckl")
            for kb in range(4):
                P.op("sp", lambda e, kb=kb: e.dma_start(
                    out=xn_f[kb].ap(0, [[1, 512]]),
                    in_=bass.AP(cv, (s * KVR + kb * 128) * AD, [[AD, 128], [1, AD]])),
                    writes=[r_xn[kb]], dma=f"cvl{kb}")

        def cache_proc(s, cs):
            for kb in range(4):
                b = next_bank()
                for c in range(4):
                    P.op("pe", lambda e, c=c, b=b, kb=kb: e.transpose(
                        out=psf[b].ap(c * 128, [[1, 128]]), in_=merged_f.ap(kb * 512 + c * 128, [[1, 128]]),
                        identity=identf.ap(0, [[1, 128]])),
                        reads=list(r_merged) + [r_const], writes=r_bank[b])
                P.op("dve", lambda e, kb=kb, b=b: e.tensor_copy(
                    out=kT[cs].ap(kb * 128, [[512, 4], [1, 128]]), in_=psf[b].ap(0, [[128, 4], [1, 128]])),
                    reads=r_bank[b], writes=r_kT[cs])
                for par_h in range(2):
                    P.op("dve", lambda e, kb=kb, par_h=par_h: e.tensor_copy(
                        out=va[cs].ap(kb * 768 + par_h * 128, [[192, 4], [1, 64]]),
                        in_=xn_f[kb].ap(par_h * 64, [[128, 4], [1, 64]])),
                        reads=[r_xn[kb]], writes=[r_va[cs][kb]])

        def merge_phase(t):
            g = tile_geom(t)
            TTk = g["TT"]
            wco = take_unit("co")
            wao = take_unit("ao")
            oT_res = [r for ch in r_oT for r in ch]
            for m in range(8):
                bA = dense_fm(wco, m * 128, TTk, zT, list(r_zT), 4)
                bB = dense_fm(wao, m * 128, TTk, oT, oT_res, 4)
                s1 = next_scr()
                s2 = next_scr()
                P.op("dve", lambda e, m=m, bA=bA, s1=s1: e.tensor_tensor(
                    out=scr[s1].ap(0, [[1, TTk]]), in0=psf[bA].ap(0, [[1, TTk]]),
                    in1=arena.ap(m * 512, [[1, TTk]]), op=ALU.mult),
                    reads=r_bank[bA] + r_sga(m), writes=[r_scr[s1]])
                P.op("dve", lambda e, m=m, bB=bB, s2=s2: e.tensor_tensor(
                    out=scr[s2].ap(0, [[1, TTk]]), in0=psf[bB].ap(0, [[1, TTk]]),
                    in1=arena.ap(4096 + m * 512, [[1, TTk]]), op=ALU.mult),
                    reads=r_bank[bB] + r_sgb(m), writes=[r_scr[s2]])
                P.op("dve", lambda e, m=m, s1=s1, s2=s2: e.tensor_tensor(
                    out=merged.ap(m * 512, [[1, TTk]]), in0=scr[s1].ap(0, [[1, TTk]]),
                    in1=scr[s2].ap(0, [[1, TTk]]), op=ALU.add),
                    reads=[r_scr[s1], r_scr[s2]], writes=[r_merged[m]])
            done_unit()
            done_unit()

        def wout_phase(t):
            g = tile_geom(t)
            SUB, par = g["SUB"], g["par"]
            wus = [take_unit("wo0"), take_unit("wo1")]
            for sub in range(4):
                for hf in range(2):
                    b = dense_tm(wus[hf], g, sub, merged, list(r_merged), list(range(8)))
                    xs_ap = xbuf[par].ap(sub * 1024 + hf * 512, [[1, 512]], pn=SUB)
                    P.op("dve", lambda e, b=b, xs_ap=xs_ap: e.tensor_tensor(
                        out=xs_ap, in0=psf[b].ap(0, [[1, 512]], pn=SUB), in1=xs_ap, op=ALU.add),
                        reads=r_bank[b] + [r_xb[par][sub]], writes=[r_xb[par][sub]])
                rms_front(g, 1, sub)
                if sub >= 1:
                    rms_back(g, sub - 1, gffnc)
            done_unit()
            done_unit()
            rms_back(g, 3, gffnc)

        def ffn1_phase(t):
            g = tile_geom(t)
            TTk = g["TT"]
            xres = list(r_xnT)
            for i in range(11):
                if i == 4 and t + 1 < NT:
                    g1 = tile_geom(t + 1)
                    for sub in range(4):
                        rms_front(g1, 0, sub)
                wu = take_unit(f"ff{i}")
                bs = [dense_fm(wu, jj * 128, TTk, xnT, xres, 8) for jj in range(4)]
                done_unit()
                for cc in range(2):
                    c = 2 * i + cc
                    si = next_scr()
                    P.op("act", lambda e, b=bs[cc], si=si: e.activation(out=scr[si].ap(0, [[1, TTk]]),
                                                                        in_=psf[b].ap(0, [[1, TTk]]), func=AF.Silu),
                         reads=r_bank[bs[cc]], writes=[r_scr[si]])
                    P.op("dve", lambda e, b=bs[2 + cc], si=si, c=c: e.tensor_tensor(
                        out=arena_bf.ap(c * 512, [[1, TTk]]), in0=psf[b].ap(0, [[1, TTk]]),
                        in1=scr[si].ap(0, [[1, TTk]]), op=ALU.mult),
                        reads=r_bank[bs[2 + cc]] + [r_scr[si]], writes=r_act(c))

        def ffn2_phase(t):
            g = tile_geom(t)
            SUB, par = g["SUB"], g["par"]
            act_res = [r_gran[c] for c in range(22)]
            for hf in range(2):
                banks = [hf * 4 + sub for sub in range(4)]
                for kg, (k0, nk) in enumerate(KG):
                    wu = take_unit(f"w2_{hf}_{kg}")
                    for sub in range(4):
                        dense_tm(wu, g, sub, arena_bf, act_res, list(range(nk)), bank=banks[sub],
                                 first=(kg == 0), last=(kg == 2), kc_base=k0)
                    done_unit()
                for sub in range(4):
                    b = banks[sub]
                    xs_ap = xbuf[par].ap(sub * 1024 + hf * 512, [[1, 512]], pn=SUB)
                    P.op("dve", lambda e, b=b, xs_ap=xs_ap: e.tensor_tensor(
                        out=xs_ap, in0=psf[b].ap(0, [[1, 512]], pn=SUB), in1=xs_ap, op=ALU.add),
                        reads=r_bank[b] + [r_xb[par][sub]], writes=[r_xb[par][sub]])
            for sub in range(4):
                if dbg < 8:
                    break
                xap, rstd, rst = rms_stats(g, 2, sub, jt=zT, jres=[r_zT[0], r_zT[1]])
                P.op("dve", lambda e, xap=xap, rstd=rstd: e.scalar_tensor_tensor(
                    out=xap, in0=xap, scalar=rstd, in1=gfin.ap(0, [[1, 1024]], pn=SUB),
                    op0=ALU.mult, op1=ALU.mult),
                    reads=[r_xb[par][sub], rst, r_const], writes=[r_xb[par][sub]])
                if g["kind"] == "p":
                    d_ap = bass.AP(yp, (g["seq"] * SEQ + g["ti"] * 512 + sub * 128) * D, [[D, 128], [1, D]])
                else:
                    d_ap = bass.AP(ys, sub * SL * D, [[D, 64], [1, D]])
                if dbg < 9:
                    continue
                P.op("pool", lambda e, xap=xap, d_ap=d_ap: e.dma_start(out=d_ap, in_=xap),
                     reads=[r_xb[par][sub]], dma=f"y{par}{sub}", is_out=True)

        if dbg >= 1:
            load_x(0)
        for _ in range(NS):
            issue_load()
        setup_bias()
        if dbg >= 1:
            g0 = tile_geom(0)
            for sub in range(4):
                rms_to_T(g0, 0, sub, gmixc)
        def skip_units(n):
            for _ in range(n):
                wstate["cur"] += 1
                done_unit()

        for t in range(NT):
            if dbg >= 2:
                proj_phase(t)
            else:
                skip_units(11)
            if t + 1 < NT and dbg >= 2:
                load_x(t + 1)
            if dbg >= 3:
                if not state.get("mask_done"):
                    P.op("dve", lambda e: e.memset(biasT.ap(0, [[256, 8], [1, 64]], p0=64, pn=64), MASKV),
                         writes=[r for hh in r_biasHJ for r in hh])
                    for hh in range(NH):
                        P.op("dve", lambda e, hh=hh: e.tensor_scalar(
                            out=biasT.ap(hh * 256, [[1, 256]]), in0=biasT.ap(hh * 256, [[1, 256]]),
                            scalar1=cbias.ap(hh, [[1, 1]]), scalar2=None, op0=ALU.subtract),
                            reads=[r_const], writes=[r for hx in r_biasHJ for r in hx])
                    state["mask_done"] = True
                if tiles[t][0] == "p" and DBG_NEWATT:
                    attention_prompt(t)
                else:
                    attention_phase(t)

            if dbg >= 4:
                merge_phase(t)
            else:
                skip_units(2)
            if dbg >= 5:
                wout_phase(t)
            else:
                skip_units(2)
            if dbg >= 6:
                ffn1_phase(t)
            else:
                skip_units(11)
            if t + 1 < NT and dbg >= 2:
                g1 = tile_geom(t + 1)
                for sub in range(4):
                    if dbg < 6:
                        rms_front(g1, 0, sub)
                    rms_back(g1, sub, gmixc)
            if dbg >= 7:
                if t + 1 < NT and tiles[t + 1][0] == "s":
                    cache_dma(0)
                    state["cache_early_done"] = True
                ffn2_phase(t)
            else:
                skip_units(6)
        if dbg >= 7:
            assert wstate["cur"] == total_units

        last_outs = [P.dma_last[k] for k in sorted(P.out_keys)]
        fin = P.op("pool", lambda e: e.memset(epsc.ap(0, [[1, 1]]), EPS))
        fin.deps = set(last_outs)

        P.finalize()

        eng_sems = {e: es.enter_context(nc.semaphore(f"sem_{e}")) for e in ("pe", "act", "dve", "pool")}
        dma_sems = {k: es.enter_context(nc.semaphore(f"dsem_{k}")) for k in P.dma_cnt}
        with nc.Block() as block:
            @block.sync
            def _(e):
                P.emit("sp", e, eng_sems, dma_sems)

            @block.gpsimd
            def _(e):
                P.emit("pool", e, eng_sems, dma_sems)

            @block.scalar
            def _(e):
                P.emit("act", e, eng_sems, dma_sems)

            @block.vector
            def _(e):
                P.emit("dve", e, eng_sems, dma_sems)

            @block.tensor
            def _(e):
                P.emit("pe", e, eng_sems, dma_sems)
        allow.__exit__(None, None, None)
    return nc


_NC_CACHE = {}


def kernel(x_prompt, x_sample, state_conv, cache_k, cache_v, g_mix, w_in, w_conv, w_conv_out,
           rel_bias, w_attn_out, w_out, g_ffn, w_ff1, w_ff3, w_ff2, g_final):
    f = lambda a: np.ascontiguousarray(np.asarray(a, dtype=np.float32))
    if "nc" not in _NC_CACHE:
        _NC_CACHE["nc"] = build_program()
    nc = _NC_CACHE["nc"]
    rb = np.asarray(rel_bias, dtype=np.float32)[0]
    idx = np.minimum(np.arange(RBW) + 1, 256)
    rbp = f(rb[:, idx])
    shared = dict(g_mix=f(g_mix)[0], w_in=f(w_in)[0], w_conv=f(w_conv)[0], w_co=f(w_conv_out)[0], rbp=rbp,
                  w_ao=f(w_attn_out)[0], w_out=f(w_out)[0], g_ffn=f(g_ffn)[0], w1=f(w_ff1)[0], w3=f(w_ff3)[0],
                  w2=f(w_ff2)[0], g_fin=f(g_final))
    xpf, xsf = f(x_prompt), f(x_sample)
    scf, ckf, cvf = f(state_conv)[0], f(cache_k)[0], f(cache_v)[0]
    in_maps = []
    for c in range(NCORES):
        m = dict(shared)
        m["xp"] = xpf[c * PB:(c + 1) * PB]
        m["xs"] = xsf[c * SB:(c + 1) * SB]
        m["sconv"] = scf[c * SB:(c + 1) * SB]
        m["ck"] = np.ascontiguousarray(ckf[c * SB:(c + 1) * SB].reshape(SB, KVR, AD))
        m["cv"] = np.ascontiguousarray(cvf[c * SB:(c + 1) * SB].reshape(SB, KVR, AD))
        in_maps.append(m)
    res = run_bass_kernel_spmd(nc, in_maps, core_ids=list(range(NCORES)))
    rs = res.results
    cat = lambda k: np.concatenate([np.asarray(r[k], dtype=np.float32) for r in rs], axis=0)
    y_p = cat("yp")
    y_s = cat("ys")
    conv_p = cat("convp")[None]
    k_p = cat("kp").reshape(1, NCORES * PB, KVR, NH, 64)
    v_p = cat("vp").reshape(1, NCORES * PB, KVR, NH, 64)
    conv_s = cat("convs")[None]
    k_s = cat("ks").reshape(1, NCORES * SB, KVR, NH, 64)
    v_s = cat("vs").reshape(1, NCORES * SB, KVR, NH, 64)
    return (y_p, y_s, conv_p, k_p, v_p, conv_s, k_s, v_s)
```

```python
import numpy as np
from contextlib import ExitStack
import concourse.bass as bass
import concourse.mybir as mybir
from concourse.bass_utils import run_bass_kernel_spmd

F32 = mybir.dt.float32
BF16 = mybir.dt.bfloat16
AF = mybir.ActivationFunctionType
ALU = mybir.AluOpType

NCORES = 8
D = 1024
DFF = 2816
NH = 8
CD = 512
AD = 512
IN_DIM = 5120
SEQ = 2048
PB = 4
SB = 4
SL = 64
KVR = 512
RBW = 768
NS = 4
NSCR = 5
EPS = 1e-6
MASKV = -30000.0
import os as _os
DBG_JMIN = int(_os.environ.get('JMIN', '0'))
DBG_ATTN = int(_os.environ.get('ATTN', '9999'))
DBG_PIPE = int(_os.environ.get('PIPE', '1'))
DBG_NEWATT = int(_os.environ.get('NEWATT', '1'))


class Res:
    __slots__ = ("w", "rd")

    def __init__(self):
        self.w = None
        self.rd = {}


class Op:
    __slots__ = ("eng", "fn", "deps", "signal", "count", "idx", "is_dma", "semkey", "dma_val")


class Prog:
    ENGS = ("pe", "act", "dve", "pool", "sp")

    def __init__(self):
        self.ops = {e: [] for e in self.ENGS}
        self.dma_last = {}
        self.dma_cnt = {}
        self.out_keys = set()

    def op(self, eng, fn, reads=(), writes=(), dma=None, is_out=False):
        o = Op()
        o.eng = eng
        o.fn = fn
        o.signal = False
        o.count = 0
        o.is_dma = dma is not None
        o.semkey = dma
        o.dma_val = 0
        deps = set()
        for r in reads:
            if r.w is not None:
                deps.add(r.w)
        for w in writes:
            if w.w is not None:
                deps.add(w.w)
            for ro in w.rd.values():
                deps.add(ro)
        if o.is_dma:
            prev = self.dma_last.get(dma)
            if prev is not None:
                deps.add(prev)
            self.dma_last[dma] = o
            self.dma_cnt[dma] = self.dma_cnt.get(dma, 0) + 1
            o.dma_val = 16 * self.dma_cnt[dma]
            if is_out:
                self.out_keys.add(dma)
        o.deps = deps
        key = ("dma", dma) if o.is_dma else eng
        for r in reads:
            r.rd[key] = o
        for w in writes:
            w.w = o
            w.rd = {}
        o.idx = len(self.ops[eng])
        self.ops[eng].append(o)
        return o

    def finalize(self):
        for e in self.ENGS:
            for o in self.ops[e]:
                for d in o.deps:
                    if d.is_dma:
                        continue
                    if d.eng == "pe" and o.eng == "pe" and not o.is_dma:
                        continue
                    d.signal = True
        for e in self.ENGS:
            c = 0
            for o in self.ops[e]:
                if o.is_dma:
                    continue
                if o.signal:
                    c += 1
                    o.count = c

    def emit(self, eng_name, eng, eng_sems, dma_sems):
        waited = {}
        for o in self.ops[eng_name]:
            waits = {}
            for d in o.deps:
                if d.is_dma:
                    k = ("d", d.semkey)
                    sem, val = dma_sems[d.semkey], d.dma_val
                else:
                    if d.eng == eng_name and not o.is_dma:
                        if eng_name == "pe":
                            continue
                        if o.idx - d.idx > 3:
                            continue
                    k = ("e", d.eng)
                    sem, val = eng_sems[d.eng], d.count
                if k not in waits or waits[k][1] < val:
                    waits[k] = (sem, val)
            for k, (sem, val) in waits.items():
                if waited.get(k, -1) >= val:
                    continue
                eng.wait_ge(sem, val)
                waited[k] = val
            ins = o.fn(eng)
            if o.is_dma:
                ins.then_inc(dma_sems[o.semkey], 16)
            elif o.signal:
                ins.then_inc(eng_sems[eng_name], 1)


class TT:
    def __init__(self, h, shape):
        self.h = h
        f = 1
        for s in shape[1:]:
            f *= s
        self.F = f

    def ap(self, off, dims, p0=0, pn=128):
        return bass.AP(self.h, p0 * self.F + off, [[self.F, pn]] + [list(d) for d in dims])


def build_program(tiles_override=None, dbg=99):
    nc = bass.Bass("TRN2", target_bir_lowering=False)
    P = Prog()

    def din(name, shape, dtp=F32):
        return nc.dram_tensor(name, shape, dtp, kind="ExternalInput")

    def dout(name, shape):
        return nc.dram_tensor(name, shape, F32, kind="ExternalOutput")

    xp = din("xp", [PB, SEQ, D])
    xs = din("xs", [SB, SL, D])
    sconv = din("sconv", [SB, 2, CD])
    ck = din("ck", [SB, KVR, AD])
    cv = din("cv", [SB, KVR, AD])
    g_mix = din("g_mix", [D])
    w_in = din("w_in", [D, IN_DIM])
    w_conv = din("w_conv", [3, CD])
    w_co = din("w_co", [CD, D])
    rbp = din("rbp", [NH, RBW])
    w_ao = din("w_ao", [AD, D])
    w_out = din("w_out", [D, D])
    g_ffn = din("g_ffn", [D])
    w1 = din("w1", [D, DFF])
    w3 = din("w3", [D, DFF])
    w2 = din("w2", [DFF, D])
    g_fin = din("g_fin", [D])
    yp = dout("yp", [PB, SEQ, D])
    ys = dout("ys", [SB, SL, D])
    convp = dout("convp", [PB, 2, CD])
    kp = dout("kp", [PB, KVR, AD])
    vp = dout("vp", [PB, KVR, AD])
    convs = dout("convs", [SB, 2, CD])
    ks = dout("ks", [SB, KVR, AD])
    vs = dout("vs", [SB, KVR, AD])

    units = []
    units.append(("q", 8, 512, [(w_in, 0, 1536, 512, 0)]))
    units.append(("k", 8, 512, [(w_in, 0, 2048, 512, 0)]))
    units.append(("v", 8, 512, [(w_in, 0, 2560, 512, 0)]))
    for c in range(4):
        units.append((f"conv{c}", 8, 384, [(w_in, 0, 512 + 128 * c, 128, 0),
                                           (w_in, 0, 1024 + 128 * c, 128, 128),
                                           (w_in, 0, 128 * c, 128, 256)]))
    for g in range(4):
        units.append((f"g{g}", 8, 512, [(w_in, 0, 3072 + 512 * g, 512, 0)]))
    units.append(("co", 4, 1024, [(w_co, 0, 0, 1024, 0)]))
    units.append(("ao", 4, 1024, [(w_ao, 0, 0, 1024, 0)]))
    for hf in range(2):
        units.append((f"wo{hf}", 8, 512, [(w_out, 0, 512 * hf, 512, 0)]))
    for i in range(11):
        units.append((f"ff{i}", 8, 512, [(w1, 0, 256 * i, 256, 0), (w3, 0, 256 * i, 256, 256)]))
    KG = [(0, 8), (8, 8), (16, 6)]
    for hf in range(2):
        for kg, (k0, nk) in enumerate(KG):
            units.append((f"w2_{hf}_{kg}", nk, 512, [(w2, 128 * k0, 512 * hf, 512, 0)]))
    NU = len(units)
    UIDX = {u[0]: i for i, u in enumerate(units)}
    src_rowlen = {id(w_in): IN_DIM, id(w_co): D, id(w_ao): D, id(w_out): D, id(w1): DFF, id(w3): DFF, id(w2): D}

    wsc = nc.dram_tensor("wsc", [NU, 128, 4096], BF16, kind="Internal")
    Sdr = nc.dram_tensor("Sdr", [NH, 128, RBW], F32, kind="Internal")

    with ExitStack() as es:
        def sbt(name, shape, dtp):
            return TT(es.enter_context(nc.sbuf_tensor(name, shape, dtp)), shape)

        xbuf = [sbt(f"xbuf{i}", [128, 4, 1024], F32) for i in range(2)]
        xn = [sbt(f"xn{i}", [128, 1024], BF16) for i in range(4)]
        xnT = sbt("xnT", [128, 8, 512], BF16)
        qT = sbt("qT", [128, 4, 2, 512], BF16)
        kT = [sbt(f"kT{i}", [128, 4, 512], BF16) for i in range(3)]
        va = [sbt(f"va{i}", [128, 4, 768], BF16) for i in range(3)]
        ubuf = [sbt(f"u{i}", [128, 528], F32) for i in range(4)]
        zT = sbt("zT", [128, 4, 512], BF16)
        oT = sbt("oT", [128, 4, 512], BF16)
        arena_h = es.enter_context(nc.sbuf_tensor("arena", [128, 8192], F32))
        arena = TT(arena_h, [128, 8192])
        arena_bf = TT(arena_h.bitcast(BF16), [128, 16384])
        biasT = sbt("biasT", [128, 8, 2, 128], F32)
        cbias = sbt("cbias", [128, 8], F32)
        sS = [sbt(f"sS{i}", [128, 2, 128], F32) for i in range(3)]
        pT = [sbt(f"pT{i}", [128, 5, 128], BF16) for i in range(4)]
        merged = sbt("merged", [128, 8, 512], BF16)
        scr = [sbt(f"scr{i}", [128, 512], F32) for i in range(NSCR)]
        wring = [sbt(f"wring{i}", [128, 4096], BF16) for i in range(NS)]
        gfin = sbt("gfin", [128, 1024], F32)
        ident = sbt("ident", [128, 128], BF16)
        gmixc = sbt("gmixc", [128, 8], F32)
        gffnc = sbt("gffnc", [128, 8], F32)
        wcT = sbt("wcT", [128, 3, 4], F32)
        epsc = sbt("epsc", [128, 1], F32)
        onesc = sbt("onesc", [128, 1], F32)
        stats = [[sbt(f"st{n}_{s}", [128, 4], F32) for s in range(4)] for n in range(3)]
        identf = sbt("identf", [128, 128], F32)
        rows = sbt("rows", [16, 520], F32)
        cst = sbt("cst", [128, 4, 8], F32)

        psf = []
        psb = []
        for i in range(8):
            h = es.enter_context(nc.psum_tensor(f"ps{i}", [128, 512], F32))
            psf.append(TT(h, [128, 512]))
            psb.append(TT(h.bitcast(BF16), [128, 1024]))

        R = lambda: Res()
        r_xb = [[R() for _ in range(4)] for _ in range(2)]
        r_xn = [R() for _ in range(4)]
        r_junk = R()
        r_biasHJ = [[R(), R()] for _ in range(NH)]
        r_biasH = r_biasHJ
        r_SdrH = [R() for _ in range(NH)]
        r_rows = R()
        r_sconv = R()
        r_cst = R()
        r_crow = R()
        r_xnT = [R() for _ in range(4)]
        r_qT = [R() for _ in range(4)]
        r_kT = [[R() for _ in range(4)] for _ in range(3)]
        r_va = [[R() for _ in range(4)] for _ in range(3)]
        r_u = [R() for _ in range(4)]
        r_zT = [R() for _ in range(4)]
        r_oT = [[R() for _ in range(4)] for _ in range(4)]
        r_gran = [R() for _ in range(32)]
        r_sga = lambda m: [r_gran[2 * m], r_gran[2 * m + 1]]
        r_sgb = lambda m: [r_gran[16 + 2 * m], r_gran[17 + 2 * m]]
        r_act = lambda c: [r_gran[c]]
        r_sS = [R(), R(), R()]
        r_pT = [R(), R(), R(), R()]
        r_rsum = [R(), R()]
        r_merged = [R() for _ in range(8)]
        r_scr = [R() for _ in range(NSCR)]
        r_wslot = [R() for _ in range(NS)]
        r_wsc = [[] for _ in range(NU)]
        r_const = R()
        r_stats = [[R() for _ in range(4)] for _ in range(3)]
        r_kctok = R()
        r_Sdr = R()
        r_bank = [[R()] for _ in range(8)]
        Sbanks = [(2, 3), (4, 5)]
        r_S = [r_bank[2] + r_bank[3], r_bank[4] + r_bank[5]]
        Oloc = [(6, 0), (7, 0)]
        r_O = [r_bank[6], r_bank[7]]

        state = {"bank": 0, "scr": 0, "xn": 0, "sb": 0, "ob": 0, "cvk": 0, "nb": 8}

        def next_bank():
            b = state["bank"] % state["nb"]
            state["bank"] = (b + 1) % state["nb"]
            return b

        def next_scr():
            s = state["scr"]
            state["scr"] = (s + 1) % NSCR
            return s

        allow = nc.allow_non_contiguous_dma(reason="small one-time / strided loads")
        allow.__enter__()

        P.op("pool", lambda e: e.memset(scr[0].ap(0, [[1, 128]]), 0.0), writes=[r_scr[0]])
        P.op("pool", lambda e: e.affine_select(out=scr[0].ap(0, [[1, 128]]), in_=scr[0].ap(0, [[1, 128]]),
                                               pattern=[[-1, 128]], compare_op=ALU.not_equal, fill=1.0,
                                               base=0, channel_multiplier=1),
             reads=[r_scr[0]], writes=[r_scr[0]])
        P.op("dve", lambda e: e.tensor_copy(out=ident.ap(0, [[1, 128]]), in_=scr[0].ap(0, [[1, 128]])),
             reads=[r_scr[0]], writes=[r_const])
        P.op("dve", lambda e: e.tensor_copy(out=identf.ap(0, [[1, 128]]), in_=scr[0].ap(0, [[1, 128]])),
             reads=[r_scr[0]], writes=[r_const])
        P.op("pool", lambda e: e.memset(epsc.ap(0, [[1, 1]]), EPS), writes=[r_const])
        P.op("pool", lambda e: e.memset(onesc.ap(0, [[1, 1]]), -1.0), writes=[r_const])
        P.op("pool", lambda e: e.memset(qT.ap(0, [[1, 4096]]), 0.0), writes=r_qT)
        for i in range(3):
            P.op("pool", lambda e, i=i: e.memset(va[i].ap(64, [[768, 4], [192, 4], [1, 64]]), 1.0),
                 writes=r_va[i])
        for i in range(2):
            P.op("pool", lambda e, i=i: e.memset(pT[i].ap(64, [[1, 64]], p0=0, pn=64), 0.0), writes=[r_pT[i]])

        P.op("sp", lambda e: e.dma_start(out=rows.ap(0, [[1, 128]], pn=8), in_=bass.AP(g_mix, 0, [[128, 8], [1, 128]])),
             writes=[r_rows], dma="c0")
        P.op("sp", lambda e: e.dma_start(out=rows.ap(128, [[1, 128]], pn=8), in_=bass.AP(g_ffn, 0, [[128, 8], [1, 128]])),
             writes=[r_rows], dma="c1")
        P.op("sp", lambda e: e.dma_start(out=rows.ap(256, [[1, 128]], pn=12), in_=bass.AP(w_conv, 0, [[128, 12], [1, 128]])),
             writes=[r_rows], dma="c2")
        P.op("sp", lambda e: e.dma_start(out=rows.ap(384, [[1, 8]], pn=1), in_=bass.AP(rbp, RBW - 1, [[0, 1], [RBW, 8]])),
             writes=[r_rows], dma="c3")
        P.op("sp", lambda e: e.dma_start(out=gfin.ap(0, [[1, 1024]]), in_=bass.AP(g_fin, 0, [[0, 128], [1, 1024]])),
             writes=[r_const], dma="c0")
        P.op("dve", lambda e: e.memset(rows.ap(392, [[1, 128]], pn=1), 1.0), writes=[r_rows])
        bpro = 0
        P.op("pe", lambda e: e.transpose(out=psf[bpro].ap(0, [[1, 8]]), in_=rows.ap(0, [[1, 128]], pn=8),
                                         identity=identf.ap(0, [[1, 8]], pn=8)),
             reads=[r_rows, r_const], writes=r_bank[bpro])
        P.op("pe", lambda e: e.transpose(out=psf[bpro].ap(8, [[1, 8]]), in_=rows.ap(128, [[1, 128]], pn=8),
                                         identity=identf.ap(0, [[1, 8]], pn=8)),
             reads=[r_rows, r_const], writes=r_bank[bpro])
        P.op("pe", lambda e: e.transpose(out=psf[bpro].ap(16, [[1, 12]]), in_=rows.ap(256, [[1, 128]], pn=12),
                                         identity=identf.ap(0, [[1, 12]], pn=12)),
             reads=[r_rows, r_const], writes=r_bank[bpro])
        P.op("pe", lambda e: e.matmul(psf[bpro].ap(32, [[1, 8]]), lhsT=rows.ap(392, [[1, 128]], pn=1),
                                      rhs=rows.ap(384, [[1, 8]], pn=1), start=True, stop=True),
             reads=[r_rows], writes=r_bank[bpro])
        P.op("dve", lambda e: e.tensor_copy(out=gmixc.ap(0, [[1, 8]]), in_=psf[bpro].ap(0, [[1, 8]])),
             reads=r_bank[bpro], writes=[r_const])
        P.op("dve", lambda e: e.tensor_copy(out=gffnc.ap(0, [[1, 8]]), in_=psf[bpro].ap(8, [[1, 8]])),
             reads=r_bank[bpro], writes=[r_const])
        P.op("dve", lambda e: e.tensor_copy(out=wcT.ap(0, [[1, 12]]), in_=psf[bpro].ap(16, [[1, 12]])),
             reads=r_bank[bpro], writes=[r_const])
        P.op("dve", lambda e: e.tensor_copy(out=cbias.ap(0, [[1, 8]]), in_=psf[bpro].ap(32, [[1, 8]])),
             reads=r_bank[bpro], writes=[r_const])
        state["bank"] = 1

        def setup_bias():
            for h in range(NH):
                P.op("sp", lambda e, h=h: e.dma_start(out=bass.AP(Sdr, h * 128 * RBW, [[RBW, 128], [1, RBW]]),
                                                      in_=bass.AP(rbp, h * RBW, [[0, 128], [1, RBW]])),
                     writes=[r_SdrH[h]], dma=f"sb{h}")
            for h in range(NH):
                for jj, j in enumerate((4, 3)):
                    off = h * 128 * RBW + 128 * (4 - j) + 127
                    P.op("sp", lambda e, h=h, jj=jj, off=off: e.dma_start(
                        out=biasT.ap((h * 2 + jj) * 128, [[1, 128]]),
                        in_=bass.AP(Sdr, off, [[RBW - 1, 128], [1, 128]])),
                        reads=[r_SdrH[h]], writes=[r_biasHJ[h][jj]], dma=f"sb{h}_{jj}")

        cvi = 0
        for ui, (uname, nk, ncols, pieces) in enumerate(units):
            for (src, row0, col0, npc, dcol0) in pieces:
                rl = src_rowlen[id(src)]
                nsplit = 8 if ui < 3 else (2 if ui < 7 else 1)
                kstep = nk // nsplit
                for sp_ in range(nsplit):
                    k0_ = sp_ * kstep
                    src_ap = bass.AP(src, (row0 + 128 * k0_) * rl + col0, [[rl, 128], [128 * rl, kstep], [1, npc]])
                    dst_ap = bass.AP(wsc, ui * 128 * 4096 + k0_ * ncols + dcol0, [[4096, 128], [ncols, kstep], [1, npc]])
                    pr = Res()
                    r_wsc[ui].append(pr)
                    P.op("pool", lambda e, s=src_ap, d=dst_ap: e.dma_start(out=d, in_=s),
                         writes=[pr], dma=f"cv{cvi % 16}")
                    cvi += 1

        for (dst, src, key) in ((ks, ck, "pk"), (vs, cv, "pv")):
            P.op("act", lambda e, dst=dst, src=src: e.dma_start(
                out=bass.AP(dst, 0, [[KVR * AD, SB], [1, (KVR - SL) * AD]]),
                in_=bass.AP(src, SL * AD, [[KVR * AD, SB], [1, (KVR - SL) * AD]])),
                dma=key, is_out=True)

        tiles = [("p", s, ti) for s in range(PB) for ti in range(4)] + [("s", 0, 0)]
        if tiles_override is not None:
            tiles = tiles_override
        NT = len(tiles)
        wstate = {"issued": 0, "cur": 0}
        total_units = NT * NU

        def issue_load():
            n = wstate["issued"]
            if n >= total_units:
                return
            ui = n % NU
            slot = n % NS
            nk, ncols = units[ui][1], units[ui][2]
            sz = nk * ncols
            P.op("sp", lambda e, slot=slot, ui=ui, sz=sz: e.dma_start(
                out=wring[slot].ap(0, [[1, sz]]), in_=bass.AP(wsc, ui * 128 * 4096, [[4096, 128], [1, sz]])),
                reads=r_wsc[ui], writes=[r_wslot[slot]], dma=f"w{slot}")
            wstate["issued"] = n + 1

        class WUnit:
            def __init__(self, n):
                self.slot = n % NS
                ui = n % NU
                self.name = units[ui][0]
                self.nk, self.ncols = units[ui][1], units[ui][2]
                self.res = r_wslot[self.slot]

            def ap(self, kc, c0, n):
                return wring[self.slot].ap(kc * self.ncols + c0, [[1, n]])

        def take_unit(name):
            n = wstate["cur"]
            wu = WUnit(n)
            assert wu.name == name, (wu.name, name)
            wstate["cur"] = n + 1
            return wu

        def done_unit():
            issue_load()

        def tile_geom(t):
            kind, seq, ti = tiles[t]
            if kind == "p":
                return dict(kind=kind, seq=seq, ti=ti, SUB=128, TT=512, nseg=1, L=512, par=t % 2)
            return dict(kind=kind, seq=0, ti=0, SUB=64, TT=256, nseg=4, L=64, par=t % 2)

        def load_x(t):
            g = tile_geom(t)
            par = g["par"]
            if g["kind"] == "p":
                src = bass.AP(xp, (g["seq"] * SEQ + g["ti"] * 512) * D, [[D, 128], [128 * D, 4], [1, D]])
                dstap = xbuf[par].ap(0, [[1024, 4], [1, 1024]])
            else:
                src = bass.AP(xs, 0, [[D, 64], [SL * D, 4], [1, D]])
                dstap = xbuf[par].ap(0, [[1024, 4], [1, 1024]], pn=64)
            P.op("sp", lambda e: e.dma_start(out=dstap, in_=src), writes=r_xb[par], dma=f"x{par}")

        def rms_stats(g, norm, sub, jt=None, jres=None):
            SUB, par = g["SUB"], g["par"]
            st = stats[norm][sub]
            rst = r_stats[norm][sub]
            xap = xbuf[par].ap(sub * 1024, [[1, 1024]], pn=SUB)
            P.op("act", lambda e: e.activation(out=jt.ap(0, [[1, 1024]], pn=SUB), in_=xap, func=AF.Square,
                                               accum_out=st.ap(0, [[1, 1]], pn=SUB)),
                 reads=[r_xb[par][sub], rst], writes=jres + [rst])
            P.op("act", lambda e: e.activation(out=st.ap(1, [[1, 1]], pn=SUB), in_=st.ap(0, [[1, 1]], pn=SUB),
                                               func=AF.Ln, scale=1.0 / D, bias=epsc.ap(0, [[1, 1]], pn=SUB)),
                 reads=[rst, r_const], writes=[rst])
            P.op("act", lambda e: e.activation(out=st.ap(2, [[1, 1]], pn=SUB), in_=st.ap(1, [[1, 1]], pn=SUB),
                                               func=AF.Exp, scale=-0.5),
                 reads=[rst], writes=[rst])
            return xap, st.ap(2, [[1, 1]], pn=SUB), rst

        def rms_front(g, norm, sub):
            SUB, par = g["SUB"], g["par"]
            xap, rstd, rst = rms_stats(g, norm, sub, jt=xn[sub], jres=[r_xn[sub]])
            P.op("act", lambda e: e.activation(out=xn[sub].ap(0, [[1, 1024]], pn=SUB), in_=xap, func=AF.Copy,
                                               scale=rstd),
                 reads=[r_xb[par][sub], rst], writes=[r_xn[sub]])

        def rms_back(g, sub, gcol):
            SUB = g["SUB"]
            b = next_bank()
            for kc in range(8):
                P.op("pe", lambda e, kc=kc: e.transpose(out=psb[b].ap(kc * SUB, [[1, SUB]]),
                                                        in_=xn[sub].ap(kc * 128, [[1, 128]], pn=SUB),
                                                        identity=ident.ap(0, [[1, SUB]], pn=SUB)),
                     reads=[r_xn[sub], r_const], writes=r_bank[b])
            P.op("dve", lambda e: e.tensor_tensor(out=xnT.ap(sub * SUB, [[512, 8], [1, SUB]]),
                                                  in0=psb[b].ap(0, [[SUB, 8], [1, SUB]]),
                                                  in1=gcol.ap(0, [[1, 8], [0, SUB]]), op=ALU.mult),
                 reads=r_bank[b] + [r_const], writes=[r_xnT[sub]])

        def rms_to_T(g, norm, sub, gcol):
            rms_front(g, norm, sub)
            rms_back(g, sub, gcol)

        def dense_fm(wu, c0, TTk, rhs_t, rhs_res, nk):
            b = next_bank()
            for kc in range(nk):
                P.op("pe", lambda e, kc=kc: e.matmul(psf[b].ap(0, [[1, TTk]]), lhsT=wu.ap(kc, c0, 128),
                                                     rhs=rhs_t.ap(kc * 512, [[1, TTk]]),
                                                     start=(kc == 0), stop=(kc == nk - 1)),
                     reads=[wu.res] + rhs_res, writes=r_bank[b])
            return b

        def dense_tm(wu, g, sub, lhs_t, lhs_res, kcs, bank=None, first=True, last=True, kc_base=0):
            SUB = g["SUB"]
            b = next_bank() if bank is None else bank
            n = len(kcs)
            for i, kc in enumerate(kcs):
                P.op("pe", lambda e, i=i, kc=kc: e.matmul(
                    psf[b].ap(0, [[1, 512]], pn=SUB),
                    lhsT=lhs_t.ap((kc_base + kc) * 512 + sub * SUB, [[1, SUB]]),
                    rhs=wu.ap(kc, 0, 512),
                    start=(first and i == 0), stop=(last and i == n - 1)),
                    reads=[wu.res] + lhs_res, writes=r_bank[b])
            return b

        def attn_unit_qk(h, NQ, qcol0, blocks):
            sb = state["sb"]
            state["sb"] = 1 - sb
            hp, hc = h % 2, h // 2
            for (j, kt, kcol0, nk, vt, voff, vres, kres) in blocks:
                bank = Sbanks[sb][0] if j < 3 else Sbanks[sb][1]
                col = j * 128 if j < 3 else (j - 3) * 128
                P.op("pe", lambda e, bank=bank, col=col, kt=kt, kcol0=kcol0, nk=nk: e.matmul(
                    psf[bank].ap(col, [[1, NQ]], pn=nk),
                    lhsT=kt.ap(hc * 512 + kcol0, [[1, nk]]),
                    rhs=qT.ap((hc * 2 + hp) * 512 + qcol0, [[1, NQ]]),
                    start=True, stop=True),
                    reads=[kres, r_qT[hc]], writes=r_S[sb])
            return sb

        def attn_unit_softmax(h, NQ, blocks, sb):
            pb = sb
            js = [b[0] for b in blocks]
            nks = {b[0]: b[3] for b in blocks}
            bankA, bankB = Sbanks[sb]
            low = [j for j in js if j < 3]
            if low:
                j0, nj = low[0], len(low)
                P.op("act", lambda e, j0=j0, nj=nj: e.activation(out=pT[pb].ap(j0 * 128, [[128, nj], [1, NQ]]),
                                                                 in_=psf[bankA].ap(j0 * 128, [[128, nj], [1, NQ]]),
                                                                 func=AF.Exp, bias=cbias.ap(h, [[1, 1]])),
                     reads=r_S[sb] + [r_const], writes=[r_pT[pb]])
            high = [j for j in js if j >= 3]
            if not high:
                return
            groups = [(j, 1, nks[j]) for j in high]
            for (j0, nj, nk) in groups:
                jj = j0 - 3
                assert nj == 1
                P.op("dve", lambda e, jj=jj, nj=nj, nk=nk, j0=j0: e.tensor_tensor(
                    out=sS[sb].ap(jj * 128, [[128, nj], [1, NQ]], pn=nk),
                    in0=psf[bankB].ap(jj * 128, [[128, nj], [1, NQ]], pn=nk),
                    in1=biasT.ap((h * 2 + (4 - j0)) * 128, [[128, nj], [1, NQ]], pn=nk), op=ALU.add),
                    reads=r_S[sb] + r_biasHJ[h], writes=[r_sS[sb]])
            for (j0, nj, nk) in groups:
                jj = j0 - 3
                P.op("act", lambda e, j0=j0, jj=jj, nj=nj, nk=nk: e.activation(
                    out=pT[pb].ap(j0 * 128, [[128, nj], [1, NQ]], pn=nk),
                    in_=sS[sb].ap(jj * 128, [[128, nj], [1, NQ]], pn=nk), func=AF.Exp,
                    bias=cbias.ap(h, [[1, 1]], pn=nk)),
                    reads=[r_sS[sb], r_const], writes=[r_pT[pb]])

        def attn_unit_pv(h, NQ, blocks, sb):
            pb = sb
            ob = state["ob"]
            state["ob"] = (ob + 1) % 2
            obank, ocol = Oloc[ob]
            hp, hc = h % 2, h // 2
            mms = []
            for (j, kt, kcol0, nk, vt, voff, vres, kres) in blocks:
                if j == 0 and NQ == 128:
                    mms.append((j, 0, 128, 0, 64, vt, voff, vres))
                    mms.append((j, 64, 64, 64, 64, vt, voff, vres))
                else:
                    mms.append((j, 0, nk, 0, NQ, vt, voff, vres))
            n = len(mms)
            for i, (j, kp0, kn, q0, qn, vt, voff, vres) in enumerate(mms):
                P.op("pe", lambda e, i=i, j=j, kp0=kp0, kn=kn, q0=q0, qn=qn, vt=vt, voff=voff: e.matmul(
                    psf[obank].ap(ocol + q0, [[1, qn]]),
                    lhsT=vt.ap(voff + hc * 192 + hp * 64, [[1, 128]], p0=kp0, pn=kn),
                    rhs=pT[pb].ap(j * 128 + q0, [[1, qn]], p0=kp0, pn=kn),
                    start=(i == 0), stop=(i == n - 1), skip_group_check=True),
                    reads=[vres, r_pT[pb]], writes=r_O[ob])
            return ob

        def attn_unit_norm(h, NQ, ob, ocol0, osub):
            obank, ocol = Oloc[ob]
            hp, hc = h % 2, h // 2
            ri = next_scr()
            srow = 64 if hp == 0 else 0
            drow = 0 if hp == 0 else 64
            P.op("dve", lambda e: e.reciprocal(out=scr[ri].ap(0, [[1, NQ]], p0=drow, pn=64),
                                               in_=psf[obank].ap(ocol, [[1, NQ]], p0=srow, pn=64)),
                 reads=r_O[ob], writes=[r_scr[ri]])
            P.op("dve", lambda e: e.tensor_tensor(out=oT.ap(hc * 512 + ocol0, [[1, NQ]], p0=drow, pn=64),
                                                  in0=psf[obank].ap(ocol, [[1, NQ]], p0=drow, pn=64),
                                                  in1=scr[ri].ap(0, [[1, NQ]], p0=drow, pn=64), op=ALU.mult),
                 reads=r_O[ob] + [r_scr[ri]], writes=[r_oT[hc][osub]])

        def run_attention(unit_list, hooks=None):
            unit_list = unit_list[:DBG_ATTN]
            n = len(unit_list)
            sbs = [None] * n
            obs = [None] * n
            if DBG_PIPE == 0:
                for i in range(n):
                    if hooks and i in hooks:
                        hooks[i]()
                    u = unit_list[i]
                    sbs[i] = attn_unit_qk(u["h"], u["NQ"], u["qcol0"], u["blocks"])
                    attn_unit_softmax(u["h"], u["NQ"], u["blocks"], sbs[i])
                    obs[i] = attn_unit_pv(u["h"], u["NQ"], u["blocks"], sbs[i])
                    attn_unit_norm(u["h"], u["NQ"], obs[i], u["ocol0"], u["osub"])
                return
            if DBG_PIPE == 2:
                for i in range(n + 1):
                    if hooks and i in hooks:
                        hooks[i]()
                    if i < n:
                        u = unit_list[i]
                        sbs[i] = attn_unit_qk(u["h"], u["NQ"], u["qcol0"], u["blocks"])
                        attn_unit_softmax(u["h"], u["NQ"], u["blocks"], sbs[i])
                    if i >= 1:
                        u = unit_list[i - 1]
                        obs[i - 1] = attn_unit_pv(u["h"], u["NQ"], u["blocks"], sbs[i - 1])
                        attn_unit_norm(u["h"], u["NQ"], obs[i - 1], u["ocol0"], u["osub"])
                return
            for i in range(n + 2):
                if hooks and i in hooks:
                    hooks[i]()
                if i < n:
                    u = unit_list[i]
                    sbs[i] = attn_unit_qk(u["h"], u["NQ"], u["qcol0"], u["blocks"])
                    attn_unit_softmax(u["h"], u["NQ"], u["blocks"], sbs[i])
                if 0 <= i - 1 < n:
                    u = unit_list[i - 1]
                    obs[i - 1] = attn_unit_pv(u["h"], u["NQ"], u["blocks"], sbs[i - 1])
                if 0 <= i - 2 < n:
                    u = unit_list[i - 2]
                    attn_unit_norm(u["h"], u["NQ"], obs[i - 2], u["ocol0"], u["osub"])

        def kv_store(g, sub, which, sidx):
            SUB = g["SUB"]
            if g["kind"] == "p":
                dst = kp if which == "k" else vp
                d_ap = bass.AP(dst, (g["seq"] * KVR + sub * 128) * AD, [[AD, 128], [1, AD]])
            else:
                dst = ks if which == "k" else vs
                d_ap = bass.AP(dst, (sub * KVR + (KVR - SL)) * AD, [[AD, 64], [1, AD]])
            P.op("pool", lambda e: e.dma_start(out=d_ap, in_=scr[sidx].ap(0, [[1, 512]], pn=SUB)),
                 reads=[r_scr[sidx]], dma=f"kv{sidx}", is_out=True)

        def proj_phase(t):
            g = tile_geom(t)
            SUB, TTk, par = g["SUB"], g["TT"], g["par"]
            prompt = g["kind"] == "p"
            cur = (g["ti"] % 2) if prompt else 2
            emit_kv = (not prompt) or g["ti"] == 3
            xres = list(r_xnT)
            wu = take_unit("q")
            for m in range(4):
                b = dense_fm(wu, m * 128, TTk, xnT, xres, 8)
                for hp_ in range(2):
                    P.op("act", lambda e, m=m, b=b, hp_=hp_: e.activation(
                        out=qT.ap((m * 2 + hp_) * 512, [[1, TTk]], p0=hp_ * 64, pn=64),
                        in_=psf[b].ap(0, [[1, TTk]], p0=hp_ * 64, pn=64), func=AF.Copy, scale=0.125),
                        reads=r_bank[b], writes=[r_qT[m]])
            done_unit()
            wu = take_unit("k")
            for m in range(4):
                b = dense_fm(wu, m * 128, TTk, xnT, xres, 8)
                P.op("dve", lambda e, m=m, b=b: e.tensor_copy(out=kT[cur].ap(m * 512, [[1, TTk]]),
                                                              in_=psf[b].ap(0, [[1, TTk]])),
                     reads=r_bank[b], writes=[r_kT[cur][m]])
            if emit_kv:
                for sub in range(4):
                    b = dense_tm(wu, g, sub, xnT, xres, list(range(8)))
                    si = next_scr()
                    P.op("act", lambda e, b=b, si=si: e.activation(out=scr[si].ap(0, [[1, 512]], pn=SUB),
                                                                   in_=psf[b].ap(0, [[1, 512]], pn=SUB), func=AF.Copy),
                         reads=r_bank[b], writes=[r_scr[si]])
                    kv_store(g, sub, "k", si)
            done_unit()
            wu = take_unit("v")
            for sub in range(4):
                b = dense_tm(wu, g, sub, xnT, xres, list(range(8)))
                for par_h in range(2):
                    P.op("dve", lambda e, b=b, sub=sub, par_h=par_h: e.tensor_copy(
                        out=va[cur].ap(sub * 768 + par_h * 128, [[192, 4], [1, 64]], pn=SUB),
                        in_=psf[b].ap(par_h * 64, [[128, 4], [1, 64]], pn=SUB)),
                        reads=r_bank[b], writes=[r_va[cur][sub]])
                if emit_kv:
                    si = next_scr()
                    P.op("dve", lambda e, b=b, si=si: e.tensor_copy(out=scr[si].ap(0, [[1, 512]], pn=SUB),
                                                                    in_=psf[b].ap(0, [[1, 512]], pn=SUB)),
                         reads=r_bank[b], writes=[r_scr[si]])
                    kv_store(g, sub, "v", si)
            done_unit()
            nseg, L = g["nseg"], g["L"]
            W2 = L + 2
            for c in range(4):
                wu = take_unit(f"conv{c}")
                b1 = dense_fm(wu, 0, TTk, xnT, xres, 8)
                b2 = dense_fm(wu, 128, TTk, xnT, xres, 8)
                b3 = dense_fm(wu, 256, TTk, xnT, xres, 8)
                done_unit()
                s1 = next_scr()
                P.op("act", lambda e, b1=b1, s1=s1: e.activation(out=scr[s1].ap(0, [[1, TTk]]),
                                                                 in_=psf[b1].ap(0, [[1, TTk]]), func=AF.Copy),
                     reads=r_bank[b1], writes=[r_scr[s1]])
                if prompt and g["ti"] == 0:
                    P.op("dve", lambda e, c=c: e.memset(ubuf[c].ap(0, [[1, 2]]), 0.0), writes=[r_u[c]])
                elif prompt:
                    P.op("dve", lambda e, c=c: e.tensor_copy(out=ubuf[c].ap(0, [[1, 2]]), in_=ubuf[c].ap(512, [[1, 2]])),
                         reads=[r_u[c]], writes=[r_u[c]])
                else:
                    if c == 0:
                        ssc = next_scr()
                        P.op("sp", lambda e, ssc=ssc: e.dma_start(out=scr[ssc].ap(0, [[1, 512]], pn=8),
                                                                  in_=bass.AP(sconv, 0, [[CD, 8], [1, CD]])),
                             writes=[r_scr[ssc]], dma="c3")
                        bsc = next_bank()
                        state["bsc"] = bsc
                        for c2 in range(4):
                            P.op("pe", lambda e, c2=c2, bsc=bsc, ssc=ssc: e.transpose(
                                out=psf[bsc].ap(c2 * 8, [[1, 8]]), in_=scr[ssc].ap(c2 * 128, [[1, 128]], pn=8),
                                identity=identf.ap(0, [[1, 8]], pn=8)),
                                reads=[r_scr[ssc], r_const], writes=r_bank[bsc])
                        for c2 in range(4):
                            P.op("dve", lambda e, c2=c2, bsc=bsc: e.tensor_copy(
                                out=ubuf[c2].ap(0, [[W2, 4], [1, 2]]), in_=psf[bsc].ap(c2 * 8, [[2, 4], [1, 2]])),
                                reads=r_bank[bsc], writes=[r_u[c2]])
                P.op("dve", lambda e, c=c, b2=b2, s1=s1: e.tensor_tensor(
                    out=ubuf[c].ap(2, [[W2, nseg], [1, L]]), in0=psf[b2].ap(0, [[L, nseg], [1, L]]),
                    in1=scr[s1].ap(0, [[L, nseg], [1, L]]), op=ALU.mult),
                    reads=r_bank[b2] + [r_scr[s1]], writes=[r_u[c]])
                s2 = next_scr()
                P.op("dve", lambda e, c=c, s2=s2: e.tensor_scalar(
                    out=scr[s2].ap(0, [[L, nseg], [1, L]]), in0=ubuf[c].ap(0, [[W2, nseg], [1, L]]),
                    scalar1=wcT.ap(0 * 4 + c, [[1, 1]]), scalar2=None, op0=ALU.mult),
                    reads=[r_u[c], r_const], writes=[r_scr[s2]])
                for tap in (1, 2):
                    P.op("dve", lambda e, c=c, s2=s2, tap=tap: e.scalar_tensor_tensor(
                        out=scr[s2].ap(0, [[L, nseg], [1, L]]), in0=ubuf[c].ap(tap, [[W2, nseg], [1, L]]),
                        scalar=wcT.ap(tap * 4 + c, [[1, 1]]), in1=scr[s2].ap(0, [[L, nseg], [1, L]]),
                        op0=ALU.mult, op1=ALU.add),
                        reads=[r_u[c], r_const, r_scr[s2]], writes=[r_scr[s2]])
                P.op("dve", lambda e, c=c, b3=b3, s2=s2: e.tensor_tensor(
                    out=zT.ap(c * 512, [[1, TTk]]), in0=psf[b3].ap(0, [[1, TTk]]),
                    in1=scr[s2].ap(0, [[1, TTk]]), op=ALU.mult),
                    reads=r_bank[b3] + [r_scr[s2]], writes=[r_zT[c]])
                if (prompt and g["ti"] == 3) or not prompt:
                    ncs = 2 if prompt else 8
                    if prompt:
                        P.op("dve", lambda e, c=c: e.tensor_copy(out=cst.ap(c * 8, [[1, 2]]), in_=ubuf[c].ap(512, [[1, 2]])),
                             reads=[r_u[c]], writes=[r_cst])
                    else:
                        P.op("dve", lambda e, c=c: e.tensor_copy(out=cst.ap(c * 8, [[2, 4], [1, 2]]),
                                                                 in_=ubuf[c].ap(L, [[W2, 4], [1, 2]])),
                             reads=[r_u[c]], writes=[r_cst])
                    if c == 3:
                        bcs = next_bank()
                        for c2 in range(4):
                            P.op("pe", lambda e, c2=c2, bcs=bcs, ncs=ncs: e.transpose(
                                out=psf[bcs].ap(c2 * 128, [[1, 128]], pn=ncs), in_=cst.ap(c2 * 8, [[1, ncs]]),
                                identity=identf.ap(0, [[1, 128]])),
                                reads=[r_cst, r_const], writes=r_bank[bcs])
                        sic = next_scr()
                        P.op("dve", lambda e, bcs=bcs, ncs=ncs, sic=sic: e.tensor_copy(
                            out=scr[sic].ap(0, [[1, 512]], pn=ncs), in_=psf[bcs].ap(0, [[1, 512]], pn=ncs)),
                            reads=r_bank[bcs], writes=[r_scr[sic]])
                        if prompt:
                            d_ap = bass.AP(convp, g["seq"] * 2 * CD, [[CD, 2], [1, CD]])
                        else:
                            d_ap = bass.AP(convs, 0, [[CD, 8], [1, CD]])
                        P.op("pool", lambda e, d_ap=d_ap, ncs=ncs, sic=sic: e.dma_start(
                            out=d_ap, in_=scr[sic].ap(0, [[1, 512]], pn=ncs)),
                            reads=[r_scr[sic]], dma="cs", is_out=True)
            for gi in range(4):
                wu = take_unit(f"g{gi}")
                for mm_ in range(4):
                    b = dense_fm(wu, mm_ * 128, TTk, xnT, xres, 8)
                    m = (gi % 2) * 4 + mm_
                    if gi < 2:
                        off, rr = m * 512, r_sga(m)
                    else:
                        off, rr = 4096 + m * 512, r_sgb(m)
                    P.op("act", lambda e, b=b, off=off: e.activation(out=arena.ap(off, [[1, TTk]]),
                                                                     in_=psf[b].ap(0, [[1, TTk]]), func=AF.Sigmoid),
                         reads=r_bank[b], writes=rr)
                done_unit()

        def attention_prompt(t):
            g = tile_geom(t)
            ti = g["ti"]
            cur, prev = ti % 2, (ti - 1) % 2
            qb0 = ti * 4
            kblocks = []
            for kbg in range(max(0, qb0 - 4), qb0 + 4):
                s_lo = max(kbg, qb0) - qb0
                s_hi = min(kbg + 4, qb0 + 3) - qb0
                st_ = cur if kbg >= qb0 else prev
                subs = list(range(s_lo, s_hi + 1))
                js = [kbg - (qb0 + sb_) + 4 for sb_ in subs]
                nh = sum(1 for j in js if j >= 3)
                kblocks.append(dict(kbg=kbg, set=st_, kb=kbg % 4, s_lo=s_lo, n=len(subs), nh=nh, js=js))
            Sb, Ob = (2, 3, 4, 5), (6, 7)
            LOOK = 3
            steps = [(h, kbi) for h in range(NH) for kbi in range(len(kblocks))]
            nsteps = len(steps)
            info = [None] * nsteps
            cnt = {"H": 0, "L": 0, "p": 0, "s": 0}

            def do_qk_softmax(i):
                h, kbi = steps[i]
                kbk = kblocks[kbi]
                hp, hc = h % 2, h // 2
                st_, kb, s_lo, n, nh = kbk["set"], kbk["kb"], kbk["s_lo"], kbk["n"], kbk["nh"]
                pb = cnt["p"] % 4
                cnt["p"] += 1
                sbk = Sb[cnt["H"] % 4]
                cnt["H"] += 1
                lhs = kT[st_].ap(hc * 512 + kb * 128, [[1, 128]])
                kres = r_kT[st_][hc]
                P.op("pe", lambda e: e.matmul(psf[sbk].ap(0, [[1, n * 128]]), lhsT=lhs,
                                              rhs=qT.ap((hc * 2 + hp) * 512 + s_lo * 128, [[1, n * 128]]),
                                              start=True, stop=True),
                     reads=[kres, r_qT[hc]], writes=r_bank[sbk])
                if nh:
                    jfirst = kbk["js"][0]
                    boff = (h * 2 + (4 - jfirst)) * 128
                    P.op("dve", lambda e: e.tensor_tensor(out=psf[sbk].ap(0, [[1, nh * 128]]),
                                                          in0=psf[sbk].ap(0, [[1, nh * 128]]),
                                                          in1=biasT.ap(boff, [[1, nh * 128]]), op=ALU.add),
                         reads=r_bank[sbk] + r_biasHJ[h], writes=r_bank[sbk])
                P.op("act", lambda e: e.activation(out=pT[pb].ap(0, [[1, n * 128]]),
                                                   in_=psf[sbk].ap(0, [[1, n * 128]]), func=AF.Exp,
                                                   bias=cbias.ap(h, [[1, 1]])),
                     reads=r_bank[sbk] + [r_const], writes=[r_pT[pb]])
                if kbk["js"][-1] == 0:
                    P.op("pool", lambda e: e.memset(pT[pb].ap(n * 128 - 64, [[1, 64]], p0=0, pn=64), 0.0),
                         reads=[r_pT[pb]], writes=[r_pT[pb]])
                info[i] = pb

            pend = []
            cur_i = [0]

            def run_pend(upto, bank=None):
                k = 0
                while k < len(pend):
                    if pend[k][0] <= upto or (bank is not None and pend[k][1] == bank):
                        pend.pop(k)[2]()
                    else:
                        k += 1

            def do_pv(i):
                h, kbi = steps[i]
                kbk = kblocks[kbi]
                if kbi == 0:
                    run_pend(-1, bank=Ob[h % 2])
                hp, hc = h % 2, h // 2
                st_, kb, s_lo, n = kbk["set"], kbk["kb"], kbk["s_lo"], kbk["n"]
                pb = info[i]
                ob = Ob[h % 2]
                vres = r_va[st_][kb]
                voff = kb * 768 + hc * 192 + hp * 64
                first, last = (kbi == 0), (kbi == len(kblocks) - 1)
                q0 = s_lo * 128
                mms = [(0, 128, 0, n * 128)]
                for mi, (kp0, kn, c0, cn) in enumerate(mms):
                    P.op("pe", lambda e, kp0=kp0, kn=kn, c0=c0, cn=cn, mi=mi: e.matmul(
                        psf[ob].ap(q0 + c0, [[1, cn]]),
                        lhsT=va[st_].ap(voff, [[1, 128]], p0=kp0, pn=kn),
                        rhs=pT[pb].ap(c0, [[1, cn]], p0=kp0, pn=kn),
                        start=(first and mi == 0), stop=(last and mi == len(mms) - 1), skip_group_check=True),
                        reads=[vres, r_pT[pb]], writes=r_bank[ob])
                if last:
                    srow = 64 if hp == 0 else 0
                    drow = 0 if hp == 0 else 64
                    nctx = {}

                    def norm_act():
                        ri = next_scr()
                        nctx["ri"] = ri
                        rs = scr[ri]
                        P.op("act", lambda e: e.activation(out=rs.ap(0, [[1, 512]], p0=drow, pn=64),
                                                           in_=psf[ob].ap(0, [[1, 512]], p0=srow, pn=64), func=AF.Ln),
                             reads=r_bank[ob], writes=[r_scr[ri]])
                        P.op("act", lambda e: e.activation(out=rs.ap(0, [[1, 512]], p0=drow, pn=64),
                                                           in_=rs.ap(0, [[1, 512]], p0=drow, pn=64), func=AF.Exp,
                                                           scale=-1.0),
                             reads=[r_scr[ri]], writes=[r_scr[ri]])

                    def norm_dve():
                        ri = nctx["ri"]
                        rs = scr[ri]
                        P.op("dve", lambda e: e.tensor_tensor(out=oT.ap(hc * 512, [[1, 512]], p0=drow, pn=64),
                                                              in0=psf[ob].ap(0, [[1, 512]], p0=drow, pn=64),
                                                              in1=rs.ap(0, [[1, 512]], p0=drow, pn=64), op=ALU.mult),
                             reads=r_bank[ob] + [r_scr[ri]], writes=r_oT[hc])
                    pend.append([cur_i[0] + 1, ob, norm_act])
                    pend.append([cur_i[0] + 4, ob, norm_dve])

            state["nb"] = 2
            for i in range(nsteps + LOOK):
                cur_i[0] = i
                if i < nsteps:
                    do_qk_softmax(i)
                run_pend(i)
                if i >= LOOK:
                    do_pv(i - LOOK)
            run_pend(1 << 30)
            state["nb"] = 8

        def attention_phase(t):
            g = tile_geom(t)
            ul = []
            state["nb"] = 2
            if g["kind"] == "p":
                ti = g["ti"]
                cur, prev = ti % 2, (ti - 1) % 2
                for sub in range(4):
                    qbg = ti * 4 + sub
                    blocks = []
                    for j in range(5):
                        kbg = qbg - 4 + j
                        if kbg < 0 or j < DBG_JMIN:
                            continue
                        st_ = cur if kbg // 4 == ti else prev
                        kb = kbg % 4
                        blocks.append((j, kT[st_], kb * 128, 128, va[st_], kb * 768, r_va[st_][kb], None))
                    for h in range(NH):
                        bl = [(j, kt, kc0, nk, vt, voff, vres, r_kT[kt_i(kt)][h // 2]) for (j, kt, kc0, nk, vt, voff, vres, _) in blocks]
                        ul.append(dict(h=h, NQ=128, qcol0=sub * 128, blocks=bl, ocol0=sub * 128, osub=sub))
                run_attention(ul)
            else:
                if not state.get("cache_early_done"):
                    cache_dma(0)
                cache_proc(0, 0)
                cache_dma(1)
                for s in range(SB):
                    cs = s % 2
                    blocks = []
                    for j in range(4):
                        blocks.append((j, kT[cs], j * 128, 128, va[cs], j * 768, r_va[cs][j], None))
                    blocks.append((4, kT[2], s * 64, 64, va[2], s * 768, r_va[2][s], None))
                    for h in range(NH):
                        bl = [(j, kt, kc0, nk, vt, voff, vres, r_kT[kt_i(kt)][h // 2]) for (j, kt, kc0, nk, vt, voff, vres, _) in blocks]
                        ul.append(dict(h=h, NQ=64, qcol0=s * 64, blocks=bl, ocol0=s * 64, osub=s))
                run_attention(ul, hooks={5: (lambda: (cache_proc(1, 1), cache_dma(2))),
                                         11: (lambda: (cache_proc(2, 0), cache_dma(3))),
                                         18: (lambda: cache_proc(3, 1))})
            state["nb"] = 8

        def kt_i(kt):
            for i in range(3):
                if kT[i] is kt:
                    return i
            raise AssertionError

        merged_f = TT(merged.h.bitcast(F32), [128, 2048])
        xn_f = [TT(xn[i].h.bitcast(F32), [128, 512]) for i in range(4)]

        def cache_dma(s):
            P.op("sp", lambda e: e.dma_start(out=merged_f.ap(0, [[512, 4], [1, 512]]),
                                             in_=bass.AP(ck, s * KVR * AD, [[AD, 128], [128 * AD, 4], [1, AD]])),
                 writes=list(r_merged), dma="ckl")
            for kb in range(4):
                P.op("sp", lambda e, kb=kb: e.dma_start(
                    out=xn_f[kb].ap(0, [[1, 512]]),
                    in_=bass.AP(cv, (s * KVR + kb * 128) * AD, [[AD, 128], [1, AD]])),
                    writes=[r_xn[kb]], dma=f"cvl{kb}")

        def cache_proc(s, cs):
            for kb in range(4):
                b = next_bank()
                for c in range(4):
                    P.op("pe", lambda e, c=c, b=b, kb=kb: e.transpose(
                        out=psf[b].ap(c * 128, [[1, 128]]), in_=merged_f.ap(kb * 512 + c * 128, [[1, 128]]),
                        identity=identf.ap(0, [[1, 128]])),
                        reads=list(r_merged) + [r_const], writes=r_bank[b])
                P.op("dve", lambda e, kb=kb, b=b: e.tensor_copy(
                    out=kT[cs].ap(kb * 128, [[512, 4], [1, 128]]), in_=psf[b].ap(0, [[128, 4], [1, 128]])),
                    reads=r_bank[b], writes=r_kT[cs])
                for par_h in range(2):
                    P.op("dve", lambda e, kb=kb, par_h=par_h: e.tensor_copy(
                        out=va[cs].ap(kb * 768 + par_h * 128, [[192, 4], [1, 64]]),
                        in_=xn_f[kb].ap(par_h * 64, [[128, 4], [1, 64]])),
                        reads=[r_xn[kb]], writes=[r_va[cs][kb]])

        def merge_phase(t):
            g = tile_geom(t)
            TTk = g["TT"]
            wco = take_unit("co")
            wao = take_unit("ao")
            oT_res = [r for ch in r_oT for r in ch]
            for m in range(8):
                bA = dense_fm(wco, m * 128, TTk, zT, list(r_zT), 4)
                bB = dense_fm(wao, m * 128, TTk, oT, oT_res, 4)
                s1 = next_scr()
                s2 = next_scr()
                P.op("dve", lambda e, m=m, bA=bA, s1=s1: e.tensor_tensor(
                    out=scr[s1].ap(0, [[1, TTk]]), in0=psf[bA].ap(0, [[1, TTk]]),
                    in1=arena.ap(m * 512, [[1, TTk]]), op=ALU.mult),
                    reads=r_bank[bA] + r_sga(m), writes=[r_scr[s1]])
                P.op("dve", lambda e, m=m, bB=bB, s2=s2: e.tensor_tensor(
                    out=scr[s2].ap(0, [[1, TTk]]), in0=psf[bB].ap(0, [[1, TTk]]),
                    in1=arena.ap(4096 + m * 512, [[1, TTk]]), op=ALU.mult),
                    reads=r_bank[bB] + r_sgb(m), writes=[r_scr[s2]])
                P.op("dve", lambda e, m=m, s1=s1, s2=s2: e.tensor_tensor(
                    out=merged.ap(m * 512, [[1, TTk]]), in0=scr[s1].ap(0, [[1, TTk]]),
                    in1=scr[s2].ap(0, [[1, TTk]]), op=ALU.add),
                    reads=[r_scr[s1], r_scr[s2]], writes=[r_merged[m]])
            done_unit()
            done_unit()

        def wout_phase(t):
            g = tile_geom(t)
            SUB, par = g["SUB"], g["par"]
            wus = [take_unit("wo0"), take_unit("wo1")]
            for sub in range(4):
                for hf in range(2):
                    b = dense_tm(wus[hf], g, sub, merged, list(r_merged), list(range(8)))
                    xs_ap = xbuf[par].ap(sub * 1024 + hf * 512, [[1, 512]], pn=SUB)
                    P.op("dve", lambda e, b=b, xs_ap=xs_ap: e.tensor_tensor(
                        out=xs_ap, in0=psf[b].ap(0, [[1, 512]], pn=SUB), in1=xs_ap, op=ALU.add),
                        reads=r_bank[b] + [r_xb[par][sub]], writes=[r_xb[par][sub]])
                rms_front(g, 1, sub)
                if sub >= 1:
                    rms_back(g, sub - 1, gffnc)
            done_unit()
            done_unit()
            rms_back(g, 3, gffnc)

        def ffn1_phase(t):
            g = tile_geom(t)
            TTk = g["TT"]
            xres = list(r_xnT)
            for i in range(11):
                if i == 4 and t + 1 < NT:
                    g1 = tile_geom(t + 1)
                    for sub in range(4):
                        rms_front(g1, 0, sub)
                wu = take_unit(f"ff{i}")
                bs = [dense_fm(wu, jj * 128, TTk, xnT, xres, 8) for jj in range(4)]
                done_unit()
                for cc in range(2):
                    c = 2 * i + cc
                    si = next_scr()
                    P.op("act", lambda e, b=bs[cc], si=si: e.activation(out=scr[si].ap(0, [[1, TTk]]),
                                                                        in_=psf[b].ap(0, [[1, TTk]]), func=AF.Silu),
                         reads=r_bank[bs[cc]], writes=[r_scr[si]])
                    P.op("dve", lambda e, b=bs[2 + cc], si=si, c=c: e.tensor_tensor(
                        out=arena_bf.ap(c * 512, [[1, TTk]]), in0=psf[b].ap(0, [[1, TTk]]),
                        in1=scr[si].ap(0, [[1, TTk]]), op=ALU.mult),
                        reads=r_bank[bs[2 + cc]] + [r_scr[si]], writes=r_act(c))

        def ffn2_phase(t):
            g = tile_geom(t)
            SUB, par = g["SUB"], g["par"]
            act_res = [r_gran[c] for c in range(22)]
            for hf in range(2):
                banks = [hf * 4 + sub for sub in range(4)]
                for kg, (k0, nk) in enumerate(KG):
                    wu = take_unit(f"w2_{hf}_{kg}")
                    for sub in range(4):
                        dense_tm(wu, g, sub, arena_bf, act_res, list(range(nk)), bank=banks[sub],
                                 first=(kg == 0), last=(kg == 2), kc_base=k0)
                    done_unit()
                for sub in range(4):
                    b = banks[sub]
                    xs_ap = xbuf[par].ap(sub * 1024 + hf * 512, [[1, 512]], pn=SUB)
                    P.op("dve", lambda e, b=b, xs_ap=xs_ap: e.tensor_tensor(
                        out=xs_ap, in0=psf[b].ap(0, [[1, 512]], pn=SUB), in1=xs_ap, op=ALU.add),
                        reads=r_bank[b] + [r_xb[par][sub]], writes=[r_xb[par][sub]])
            for sub in range(4):
                if dbg < 8:
                    break
                xap, rstd, rst = rms_stats(g, 2, sub, jt=zT, jres=[r_zT[0], r_zT[1]])
                P.op("dve", lambda e, xap=xap, rstd=rstd: e.scalar_tensor_tensor(
                    out=xap, in0=xap, scalar=rstd, in1=gfin.ap(0, [[1, 1024]], pn=SUB),
                    op0=ALU.mult, op1=ALU.mult),
                    reads=[r_xb[par][sub], rst, r_const], writes=[r_xb[par][sub]])
                if g["kind"] == "p":
                    d_ap = bass.AP(yp, (g["seq"] * SEQ + g["ti"] * 512 + sub * 128) * D, [[D, 128], [1, D]])
                else:
                    d_ap = bass.AP(ys, sub * SL * D, [[D, 64], [1, D]])
                if dbg < 9:
                    continue
                P.op("pool", lambda e, xap=xap, d_ap=d_ap: e.dma_start(out=d_ap, in_=xap),
                     reads=[r_xb[par][sub]], dma=f"y{par}{sub}", is_out=True)

        if dbg >= 1:
            load_x(0)
        for _ in range(NS):
            issue_load()
        setup_bias()
        if dbg >= 1:
            g0 = tile_geom(0)
            for sub in range(4):
                rms_to_T(g0, 0, sub, gmixc)
        def skip_units(n):
            for _ in range(n):
                wstate["cur"] += 1
                done_unit()

        for t in range(NT):
            if dbg >= 2:
                proj_phase(t)
            else:
                skip_units(11)
            if t + 1 < NT and dbg >= 2:
                load_x(t + 1)
            if dbg >= 3:
                if not state.get("mask_done"):
                    P.op("dve", lambda e: e.memset(biasT.ap(0, [[256, 8], [1, 64]], p0=64, pn=64), MASKV),
                         writes=[r for hh in r_biasHJ for r in hh])
                    for hh in range(NH):
                        P.op("dve", lambda e, hh=hh: e.tensor_scalar(
                            out=biasT.ap(hh * 256, [[1, 256]]), in0=biasT.ap(hh * 256, [[1, 256]]),
                            scalar1=cbias.ap(hh, [[1, 1]]), scalar2=None, op0=ALU.subtract),
                            reads=[r_const], writes=[r for hx in r_biasHJ for r in hx])
                    state["mask_done"] = True
                if tiles[t][0] == "p" and DBG_NEWATT:
                    attention_prompt(t)
                else:
                    attention_phase(t)

            if dbg >= 4:
                merge_phase(t)
            else:
                skip_units(2)
            if dbg >= 5:
                wout_phase(t)
            else:
                skip_units(2)
            if dbg >= 6:
                ffn1_phase(t)
            else:
                skip_units(11)
            if t + 1 < NT and dbg >= 2:
                g1 = tile_geom(t + 1)
                for sub in range(4):
                    if dbg < 6:
                        rms_front(g1, 0, sub)
                    rms_back(g1, sub, gmixc)
            if dbg >= 7:
                if t + 1 < NT and tiles[t + 1][0] == "s":
                    cache_dma(0)
                    state["cache_early_done"] = True
                ffn2_phase(t)
            else:
                skip_units(6)
        if dbg >= 7:
            assert wstate["cur"] == total_units

        last_outs = [P.dma_last[k] for k in sorted(P.out_keys)]
        fin = P.op("pool", lambda e: e.memset(epsc.ap(0, [[1, 1]]), EPS))
        fin.deps = set(last_outs)

        P.finalize()

        eng_sems = {e: es.enter_context(nc.semaphore(f"sem_{e}")) for e in ("pe", "act", "dve", "pool")}
        dma_sems = {k: es.enter_context(nc.semaphore(f"dsem_{k}")) for k in P.dma_cnt}
        with nc.Block() as block:
            @block.sync
            def _(e):
                P.emit("sp", e, eng_sems, dma_sems)

            @block.gpsimd
            def _(e):
                P.emit("pool", e, eng_sems, dma_sems)

            @block.scalar
            def _(e):
                P.emit("act", e, eng_sems, dma_sems)

            @block.vector
            def _(e):
                P.emit("dve", e, eng_sems, dma_sems)

            @block.tensor
            def _(e):
                P.emit("pe", e, eng_sems, dma_sems)
        allow.__exit__(None, None, None)
    return nc


_NC_CACHE = {}


def kernel(x_prompt, x_sample, state_conv, cache_k, cache_v, g_mix, w_in, w_conv, w_conv_out,
           rel_bias, w_attn_out, w_out, g_ffn, w_ff1, w_ff3, w_ff2, g_final):
    f = lambda a: np.ascontiguousarray(np.asarray(a, dtype=np.float32))
    if "nc" not in _NC_CACHE:
        _NC_CACHE["nc"] = build_program()
    nc = _NC_CACHE["nc"]
    rb = np.asarray(rel_bias, dtype=np.float32)[0]
    idx = np.minimum(np.arange(RBW) + 1, 256)
    rbp = f(rb[:, idx])
    shared = dict(g_mix=f(g_mix)[0], w_in=f(w_in)[0], w_conv=f(w_conv)[0], w_co=f(w_conv_out)[0], rbp=rbp,
                  w_ao=f(w_attn_out)[0], w_out=f(w_out)[0], g_ffn=f(g_ffn)[0], w1=f(w_ff1)[0], w3=f(w_ff3)[0],
                  w2=f(w_ff2)[0], g_fin=f(g_final))
    xpf, xsf = f(x_prompt), f(x_sample)
    scf, ckf, cvf = f(state_conv)[0], f(cache_k)[0], f(cache_v)[0]
    in_maps = []
    for c in range(NCORES):
        m = dict(shared)
        m["xp"] = xpf[c * PB:(c + 1) * PB]
        m["xs"] = xsf[c * SB:(c + 1) * SB]
        m["sconv"] = scf[c * SB:(c + 1) * SB]
        m["ck"] = np.ascontiguousarray(ckf[c * SB:(c + 1) * SB].reshape(SB, KVR, AD))
        m["cv"] = np.ascontiguousarray(cvf[c * SB:(c + 1) * SB].reshape(SB, KVR, AD))
        in_maps.append(m)
    res = run_bass_kernel_spmd(nc, in_maps, core_ids=list(range(NCORES)))
    rs = res.results
    cat = lambda k: np.concatenate([np.asarray(r[k], dtype=np.float32) for r in rs], axis=0)
    y_p = cat("yp")
    y_s = cat("ys")
    conv_p = cat("convp")[None]
    k_p = cat("kp").reshape(1, NCORES * PB, KVR, NH, 64)
    v_p = cat("vp").reshape(1, NCORES * PB, KVR, NH, 64)
    conv_s = cat("convs")[None]
    k_s = cat("ks").reshape(1, NCORES * SB, KVR, NH, 64)
    v_s = cat("vs").reshape(1, NCORES * SB, KVR, NH, 64)
    return (y_p, y_s, conv_p, k_p, v_p, conv_s, k_s, v_s)
```
